# Optimizing a Trainium2 kernel written in Bass

```python
import math
import jax, jax.numpy as jnp
from jax import lax
import numpy as np

D_MODEL = 1024
BATCH = 8
SEQ = 4096
DEPTH = 1

N_META = 16
GDN_HEADS = D_MODEL // 256
GDN_DK = 128
GDN_DV = 128
FOX_HEADS = D_MODEL // 128
FOX_DH = 64
D_MIX = GDN_HEADS * GDN_DV + FOX_HEADS * FOX_DH
CONV_K = 4
CHUNK = 64
BLOCK_Q = 128
D_FF = -(-8 * D_MODEL // (3 * 256)) * 256
EPS = 1e-6
MASK_VALUE = -1e30
GDN_QKV = 2 * GDN_HEADS * GDN_DK + GDN_HEADS * GDN_DV
IN_SIZES = (GDN_HEADS * GDN_DK, GDN_HEADS * GDN_DK, GDN_HEADS * GDN_DV, GDN_HEADS * GDN_DV,
            GDN_HEADS, GDN_HEADS,
            FOX_HEADS * FOX_DH, FOX_HEADS * FOX_DH, FOX_HEADS * FOX_DH, FOX_HEADS)
D_IN = sum(IN_SIZES)
IN_SPLITS = tuple(int(s) for s in np.cumsum(IN_SIZES)[:-1])

kernel_name = "hymba_gdn_fox_hybrid_layer"


def rmsnorm(x, w):
    x32 = x.astype(jnp.float32)
    y = x32 * lax.rsqrt(jnp.mean(x32 * x32, axis=-1, keepdims=True) + EPS)
    return (y * w.astype(jnp.float32)).astype(x.dtype)


def l2norm(x):
    return x * lax.rsqrt(jnp.sum(x * x, axis=-1, keepdims=True) + EPS)


def causal_dwconv(x, w):
    k = w.shape[0]
    return lax.conv_general_dilated(x, w[:, None, :], window_strides=(1,), padding=[(k - 1, 0)],
                                    dimension_numbers=("NWC", "WIO", "NWC"),
                                    feature_group_count=x.shape[-1])


def chunk_gated_delta_rule(q, k, v, beta, g, s0, chunk):
    b, h, t, _ = q.shape
    dv = v.shape[-1]
    n = t // chunk
    q, k, v = (a.reshape(b, h, n, chunk, a.shape[-1]) for a in (q, k, v))
    beta = beta.reshape(b, h, n, chunk)
    gc = jnp.cumsum(g.reshape(b, h, n, chunk), axis=-1)
    incl = jnp.tril(jnp.ones((chunk, chunk), dtype=bool))
    strict = jnp.tril(jnp.ones((chunk, chunk), dtype=bool), -1)
    diff = gc[..., :, None] - gc[..., None, :]
    decay = jnp.where(incl, jnp.exp(jnp.where(incl, diff, 0.0)), 0.0)
    a_mat = jnp.where(strict, jnp.einsum("bhncd,bhnsd->bhncs", k, k) * decay * beta[..., :, None], 0.0)
    rhs = jnp.concatenate([v * beta[..., None], k * (beta * jnp.exp(gc))[..., None]], axis=-1)
    sol = lax.linalg.triangular_solve(a_mat + jnp.eye(chunk, dtype=a_mat.dtype), rhs,
                                      left_side=True, lower=True, unit_diagonal=True)
    u, w = sol[..., :dv], sol[..., dv:]
    qk = jnp.einsum("bhncd,bhnsd->bhncs", q, k) * decay
    xs = tuple(jnp.moveaxis(a, 2, 0) for a in (q, k, u, w, gc, qk))

    def step(state, inp):
        q_c, k_c, u_c, w_c, g_c, qk_c = inp
        v_new = u_c - jnp.einsum("bhcd,bhde->bhce", w_c, state)
        o_c = (jnp.einsum("bhcd,bhde->bhce", q_c * jnp.exp(g_c)[..., None], state)
               + jnp.einsum("bhcs,bhse->bhce", qk_c, v_new))
        g_last = g_c[..., -1:]
        state = (state * jnp.exp(g_last)[..., None]
                 + jnp.einsum("bhcd,bhce->bhde", k_c * jnp.exp(g_last - g_c)[..., None], v_new))
        return state, o_c

    s_final, o = lax.scan(step, s0, xs)
    o = jnp.moveaxis(o, 0, 2).reshape(b, h, t, dv)
    return o, s_final


def gated_deltanet_group(qkv, z, b_raw, a_raw, conv_w, a_log, dt_bias, norm_w):
    bsz, t, _ = qkv.shape
    f32 = jnp.float32
    qkv = jax.nn.silu(causal_dwconv(qkv, conv_w)).astype(f32)
    q, k, v = jnp.split(qkv, [GDN_HEADS * GDN_DK, 2 * GDN_HEADS * GDN_DK], axis=-1)
    q = l2norm(q.reshape(bsz, t, GDN_HEADS, GDN_DK)).transpose(0, 2, 1, 3) * (GDN_DK ** -0.5)
    k = l2norm(k.reshape(bsz, t, GDN_HEADS, GDN_DK)).transpose(0, 2, 1, 3)
    v = v.reshape(bsz, t, GDN_HEADS, GDN_DV).transpose(0, 2, 1, 3)
    beta = jax.nn.sigmoid(b_raw.astype(f32)).transpose(0, 2, 1)
    g = (-jnp.exp(a_log.astype(f32))
         * jax.nn.softplus(a_raw.astype(f32) + dt_bias.astype(f32))).transpose(0, 2, 1)
    s0 = jnp.zeros((bsz, GDN_HEADS, GDN_DK, GDN_DV), f32)
    o_meta, s_meta = chunk_gated_delta_rule(q[:, :, :N_META], k[:, :, :N_META], v[:, :, :N_META],
                                            beta[:, :, :N_META], g[:, :, :N_META], s0, N_META)
    o_real, _ = chunk_gated_delta_rule(q[:, :, N_META:], k[:, :, N_META:], v[:, :, N_META:],
                                       beta[:, :, N_META:], g[:, :, N_META:], s_meta, CHUNK)
    o = jnp.concatenate([o_meta, o_real], axis=2).transpose(0, 2, 1, 3)
    zg = jax.nn.silu(z.astype(f32).reshape(bsz, t, GDN_HEADS, GDN_DV))
    o = o * lax.rsqrt(jnp.mean(o * o, axis=-1, keepdims=True) + EPS) * norm_w.astype(f32) * zg
    return o.reshape(bsz, t, GDN_HEADS * GDN_DV).astype(qkv.dtype)


def forgetting_attention_group(q, k, v, f_raw, f_bias):
    bsz, t, _ = q.shape
    f32 = jnp.float32
    q, k, v = (a.astype(f32).reshape(bsz, t, FOX_HEADS, FOX_DH) for a in (q, k, v))
    logf = jax.nn.log_sigmoid(f_raw.astype(f32) + f_bias.astype(f32))
    c = jnp.cumsum(logf, axis=1).transpose(0, 2, 1)
    scale = FOX_DH ** -0.5
    n_blocks = (t - N_META) // BLOCK_Q
    bounds = [0, N_META] + [N_META + (i + 1) * BLOCK_Q for i in range(n_blocks)]
    outs = []
    for s, e in zip(bounds[:-1], bounds[1:]):
        sc = (jnp.einsum("bqhd,bkhd->bhqk", q[:, s:e], k[:, :e]) * scale
              + c[:, :, s:e, None] - c[:, :, None, :e])
        causal = jnp.arange(s, e)[:, None] >= jnp.arange(e)[None, :]
        p = jax.nn.softmax(jnp.where(causal, sc, MASK_VALUE), axis=-1)
        outs.append(jnp.einsum("bhqk,bkhd->bqhd", p, v[:, :e]))
    return jnp.concatenate(outs, axis=1).reshape(bsz, t, FOX_HEADS * FOX_DH)


def swiglu(x, w_gate, w_up, w_down):
    return (jax.nn.silu(x @ w_gate) * (x @ w_up)) @ w_down


def setup_inputs(seed: int = 0) -> dict:
    key = jax.random.key(seed)
    ks = jax.random.split(key, 15)
    nrm = jax.random.normal
    dt = jnp.exp(jax.random.uniform(ks[6], (DEPTH, GDN_HEADS), minval=math.log(1e-3), maxval=math.log(1e-1)))
    return {
        "x": nrm(ks[0], (BATCH, SEQ, D_MODEL), jnp.float32),
        "meta_tokens": nrm(ks[1], (N_META, D_MODEL), jnp.float32),
        "attn_norm_w": 1.0 + 0.02 * nrm(ks[2], (DEPTH, D_MODEL), jnp.float32),
        "w_in": nrm(ks[3], (DEPTH, D_MODEL, D_IN), jnp.float32) * D_MODEL ** -0.5,
        "conv_w": nrm(ks[4], (DEPTH, CONV_K, GDN_QKV), jnp.float32) * CONV_K ** -0.5,
        "a_log": jnp.log(jax.random.uniform(ks[5], (DEPTH, GDN_HEADS), minval=1.0, maxval=16.0)),
        "dt_bias": dt + jnp.log(-jnp.expm1(-dt)),
        "gdn_norm_w": 1.0 + 0.02 * nrm(ks[7], (DEPTH, GDN_DV), jnp.float32),
        "fgate_b": 2.0 + 0.5 * nrm(ks[8], (DEPTH, FOX_HEADS), jnp.float32),
        "w_out": nrm(ks[9], (DEPTH, D_MIX, D_MODEL), jnp.float32) * D_MIX ** -0.5,
        "ffn_norm_w": 1.0 + 0.02 * nrm(ks[10], (DEPTH, D_MODEL), jnp.float32),
        "w_gate": nrm(ks[11], (DEPTH, D_MODEL, D_FF), jnp.float32) * D_MODEL ** -0.5,
        "w_up": nrm(ks[12], (DEPTH, D_MODEL, D_FF), jnp.float32) * D_MODEL ** -0.5,
        "w_down": nrm(ks[13], (DEPTH, D_FF, D_MODEL), jnp.float32) * D_FF ** -0.5,
        "final_norm_w": 1.0 + 0.02 * nrm(ks[14], (D_MODEL,), jnp.float32),
    }


def reference(x, meta_tokens, attn_norm_w, w_in, conv_w, a_log, dt_bias, gdn_norm_w, fgate_b,
              w_out, ffn_norm_w, w_gate, w_up, w_down, final_norm_w):
    bsz = x.shape[0]
    meta = jnp.broadcast_to(meta_tokens[None].astype(x.dtype), (bsz, N_META, D_MODEL))
    h = jnp.concatenate([meta, x], axis=1)
    for l in range(DEPTH):
        u = rmsnorm(h, attn_norm_w[l])
        proj = u @ w_in[l]
        gq, gk, gv, gz, gb, ga, fq, fk, fv, ff = jnp.split(proj, IN_SPLITS, axis=-1)
        o_gdn = gated_deltanet_group(jnp.concatenate([gq, gk, gv], axis=-1), gz, gb, ga,
                                     conv_w[l], a_log[l], dt_bias[l], gdn_norm_w[l])
        o_fox = forgetting_attention_group(fq, fk, fv, ff, fgate_b[l]).astype(h.dtype)
        h = h + jnp.concatenate([o_gdn, o_fox], axis=-1) @ w_out[l]
        h = h + swiglu(rmsnorm(h, ffn_norm_w[l]), w_gate[l], w_up[l], w_down[l])
    h = rmsnorm(h, final_norm_w)
    return h[:, N_META:]
```

```python
import numpy as np
import ml_dtypes
from contextlib import ExitStack
import concourse.bass as bass
import concourse.mybir as mybir
from concourse.bass_utils import run_bass_kernel_spmd

F32 = mybir.dt.float32
BF16 = mybir.dt.bfloat16
AF = mybir.ActivationFunctionType
ALU = mybir.AluOpType

D = 1024
DIN = 3600
DFF = 2816
NFC = DFF // 128
EPS = 1e-6
NCORES = 8
SEQ = 4096


class Sched:
    ENG = ("pe", "act", "dve", "pool", "sp")

    def __init__(self, nc, es):
        self.nc = nc
        self.es = es
        self.ops = {e: [] for e in self.ENG}
        self.last_w = {}
        self.readers = {}
        self.sems = {e: es.enter_context(nc.semaphore("s_" + e)) for e in self.ENG}
        self.dsem = {}
        self.dcount = {}
        self.dlast = {}
        self.barrier_deps = []

    def op(self, eng, fn, reads=(), writes=(), dma=None):
        o = dict(eng=eng, fn=fn, deps=[], sig=False, dma=dma)
        for w in self.barrier_deps:
            self._dep(o, w)
        for r in reads:
            w = self.last_w.get(r)
            if w is not None:
                self._dep(o, w)
            if len(r) == 2 and r[0] == "P" and r[1].isdigit():
                for rd in self.readers.get(r, ()):
                    if rd["eng"] != eng:
                        self._dep(o, rd)
        for r in writes:
            w = self.last_w.get(r)
            if w is not None:
                self._dep(o, w)
            for rd in self.readers.get(r, ()):
                self._dep(o, rd)
        for r in reads:
            self.readers.setdefault(r, []).append(o)
        for r in writes:
            self.last_w[r] = o
            self.readers[r] = []
        if dma is not None:
            if dma not in self.dsem:
                self.dsem[dma] = self.es.enter_context(self.nc.semaphore("d_" + dma))
                self.dcount[dma] = 0
            self.dcount[dma] += 1
            o["dval"] = 16 * self.dcount[dma]
            self.dlast[dma] = o
        self.ops[eng].append(o)
        return o

    def _dep(self, o, w):
        if w is o:
            return
        if w["dma"] is None and o["dma"] is None and w["eng"] == "pe" and o["eng"] == "pe":
            return
        if w["dma"] is not None and o["dma"] == w["dma"] and w["dma"].startswith("T_"):
            return
        o["deps"].append(w)
        if w["dma"] is None:
            w["sig"] = True

    def barrier(self):
        deps = []
        for e in self.ENG:
            if self.ops[e]:
                deps.append(self.ops[e][-1])
        for d in self.dlast.values():
            deps.append(d)
        for w in deps:
            if w["dma"] is None:
                w["sig"] = True
        self.barrier_deps = deps

    def emit(self, block, final_waits=()):
        for e in self.ENG:
            c = 0
            for o in self.ops[e]:
                if o["dma"] is None and o["sig"]:
                    c += 1
                    o["sval"] = c
        S = self

        def run(eng_name):
            def body(eng):
                waited = {}
                for o in S.ops[eng_name]:
                    need = {}
                    for w in o["deps"]:
                        if w["dma"] is not None:
                            nm = w["dma"]
                            sem = S.dsem[nm]
                            val = 16 * S.dcount[nm] if nm.startswith("T_") else w["dval"]
                        else:
                            sem = S.sems[w["eng"]]
                            val = w["sval"]
                        k = sem.num
                        if k not in need or need[k][1] < val:
                            need[k] = (sem, val)
                    for k, (sem, val) in need.items():
                        if waited.get(k, 0) >= val:
                            continue
                        eng.wait_ge(sem, val)
                        waited[k] = val
                    ins = o["fn"](eng)
                    if o["dma"] is not None:
                        ins.then_inc(S.dsem[o["dma"]], 16)
                    elif o["sig"]:
                        ins.then_inc(S.sems[eng_name], 1)
                if eng_name == "sp":
                    for nm in final_waits:
                        eng.wait_ge(S.dsem[nm], 16 * S.dcount[nm])
            return body

        block.tensor(run("pe"))
        block.scalar(run("act"))
        block.vector(run("dve"))
        block.gpsimd(run("pool"))
        block.sync(run("sp"))


class Arena:
    def __init__(self, handle, nwords):
        self.h = handle
        self.n = nwords
        self.off = 0

    def alloc(self, free_shape, dtype):
        n = int(np.prod(free_shape))
        if dtype == BF16:
            words = (n + 1) // 2
        else:
            words = n
        assert self.off + words <= self.n, ("arena overflow", self.off, words, self.n)
        ap = self.h[:, self.off:self.off + words]
        self.off += words
        if dtype == BF16:
            ap = ap.bitcast(BF16)[:, 0:n]
        if len(free_shape) == 2:
            ap = ap.rearrange("p (a b) -> p a b", a=free_shape[0])
        elif len(free_shape) == 3:
            ap = ap.rearrange("p (a b c) -> p a b c", a=free_shape[0], b=free_shape[1])
        return ap


def MM(out, lhsT, rhs, start=True, stop=True):
    return lambda e: e.matmul(out, lhsT=lhsT, rhs=rhs, start=start, stop=stop)


def TR(out, in_, ident):
    return lambda e: e.transpose(out, in_, ident)


def ACTF(out, in_, func, bias=None, scale=None, accum_out=None):
    kw = {}
    if bias is not None:
        kw["bias"] = bias
    if scale is not None:
        kw["scale"] = scale
    if accum_out is not None:
        kw["accum_out"] = accum_out
    return lambda e: e.activation(out=out, in_=in_, func=func, **kw)


def AMUL(out, in_, mul):
    return lambda e: e.mul(out, in_, mul)


def TT(out, in0, in1, op):
    return lambda e: e.tensor_tensor(out=out, in0=in0, in1=in1, op=op)


def TS(out, in0, s1, s2, op0, op1=None):
    if op1 is None:
        return lambda e: e.tensor_scalar(out=out, in0=in0, scalar1=s1, scalar2=None, op0=op0)
    return lambda e: e.tensor_scalar(out=out, in0=in0, scalar1=s1, scalar2=s2, op0=op0, op1=op1)


def STT(out, in0, scalar, in1, op0, op1):
    return lambda e: e.scalar_tensor_tensor(out=out, in0=in0, scalar=scalar, in1=in1, op0=op0, op1=op1)


def CP(out, in_):
    return lambda e: e.tensor_copy(out=out, in_=in_)


def RCP(out, in_):
    return lambda e: e.reciprocal(out=out, in_=in_)


def MS(ap, val):
    return lambda e: e.memset(ap, val)


def DMA(out, in_):
    return lambda e: e.dma_start(out=out, in_=in_)


SRC = dict(gq=0, gk=512, gv=1024, gz=1536, gb=2048, ga=2052, fq=2056, fk=2568, fv=3080, ff=3592)
DST = dict(gq=0, gk=512, gv=1024, fq=1536, fk=2048, gz=2560, fv=3072, gb=3584, ga=3588, ff=3592)
WID = dict(gq=512, gk=512, gv=512, gz=512, gb=4, ga=4, fq=512, fk=512, fv=512, ff=8)


def build_nc(NR, phases="AB", dbg=False, stop=None, dumps=()):
    NT = NR + 1
    T = 16 + 128 * NR
    nc = bass.Bass("TRN2", target_bir_lowering=False)

    def dram(name, shape, dt=F32, kind="ExternalInput"):
        return nc.dram_tensor(name, list(shape), dt, kind=kind).ap()

    x_d = dram("x", [NR * 128, D])
    meta_d = dram("meta", [16, D])
    win_d = dram("w_in", [D, DIN])
    wout_d = dram("w_out", [D, D])
    wg_d = dram("w_gate", [D, DFF])
    wu_d = dram("w_up", [D, DFF])
    wd_d = dram("w_down", [DFF, D])
    cf_d = dram("cf32", [128, 7 * 128])
    cb_d = dram("cbf16", [128, 2 * 128], BF16)
    convw_d = dram("convw", [128, 48])
    anw_d = dram("anw", [128, 8])
    fnw_d = dram("fnw", [128, 8])
    finw_d = dram("finw", [128, D])
    gnw_d = dram("gnw", [128, 512])
    raw12_d = dram("raw12", [128, 12])
    sgn12_d = dram("sgn12", [128, 12])
    alog_d = dram("alog", [128, 4])
    out_d = dram("out", [NR * 128, D], kind="ExternalOutput")
    h1_d = dram("h1s", [NR * 128, D], kind="ExternalOutput" if dbg else "Internal")

    es = ExitStack()
    with es:
        NW = 53000
        arena_h = es.enter_context(nc.sbuf_tensor("arena", [128, NW], F32))
        AR = Arena(arena_h, NW)
        pb = [es.enter_context(nc.psum_tensor("pb%d" % i, [128, 512], F32))[:] for i in range(8)]
        pb0b = pb[0].bitcast(BF16)
        pb1b = pb[1].bitcast(BF16)
        S = Sched(nc, es)

        cf = AR.alloc([7, 128], F32)
        identf, TRI, ONES, HALF, POSA, NEGD = (cf[:, k, :] for k in range(6))
        cb = AR.alloc([2, 128], BF16)
        identb, MASK01 = cb[:, 0, :], cb[:, 1, :]
        S.op("sp", DMA(cf, cf_d.rearrange("p (a b) -> p a b", a=7)), writes=["const"], dma="T_c")
        S.op("sp", DMA(cb, cb_d.rearrange("p (a b) -> p a b", a=2)), writes=["const"], dma="T_c")
        common_mark = AR.off

        class StopBuild(Exception):
            pass

        def checkpoint(i, name, env):
            if stop is None or stop != (i, name):
                return
            for dn, fn_, reads in dumps:
                ap = fn_(env)
                d = nc.dram_tensor("dump_" + dn, list(ap.shape), ap.dtype, kind="ExternalOutput").ap()
                S.op("sp", DMA(d, ap), reads=reads, writes=["dump_" + dn], dma="T_dump")
            raise StopBuild()

        def small_load(dst, src):
            S.op("sp", DMA(dst, src), writes=["const"], dma="T_c")

        ev = [0]

        def evac_copy(out, in_, reads, writes, scale=None, eng=None):
            ev[0] += 1
            e = eng if eng is not None else ("act" if ev[0] % 2 else "dve")
            if e == "act":
                fn = ACTF(out, in_, AF.Copy) if scale is None else AMUL(out, in_, scale)
            else:
                fn = CP(out, in_) if scale is None else TS(out, in_, scale, None, ALU.mult)
            S.op(e, fn, reads=reads, writes=writes)

        UT = ["uT%d" % k for k in range(8)]
        UC = ["uc%d" % k for k in range(8)]

        if "A" in phases:
            Win = AR.alloc([8, DIN], BF16)
            Wout = AR.alloc([8, D], BF16)
            KT = AR.alloc([4, T], BF16)
            V = AR.alloc([NT, 8, 65], BF16)
            CC = AR.alloc([NT, 8], F32)
            biasT = AR.alloc([NT, 8], F32)
            convw = AR.alloc([12, 4], F32)
            anw = AR.alloc([8], F32)
            gnw = AR.alloc([4, 128], F32)
            raw12 = AR.alloc([12], F32)
            sgn12 = AR.alloc([12], F32)
            bias12 = AR.alloc([12], F32)
            mul12 = AR.alloc([12], F32)
            alog = AR.alloc([4], F32)
            accf = AR.alloc([8], F32)
            xtB = [AR.alloc([D], F32) for _ in range(2)]
            uB = [AR.alloc([D], BF16) for _ in range(2)]
            uTB = [AR.alloc([8, 128], BF16) for _ in range(2)]
            pre = AR.alloc([12, 131], F32)
            cv = AR.alloc([12, 128], F32)
            QTg = AR.alloc([4, 128], BF16)
            KTg = AR.alloc([4, 128], BF16)
            VTg = AR.alloc([4, 128], BF16)
            QTfB = [AR.alloc([4, 128], BF16) for _ in range(2)]
            zsB = [AR.alloc([4, 128], F32) for _ in range(2)]
            sm = AR.alloc([16], F32)
            gt = AR.alloc([4, 12], F32)
            sm4 = AR.alloc([9, 4], F32)
            ssq = AR.alloc([4], F32)
            so = AR.alloc([8], F32)
            cref = AR.alloc([8], F32)
            rd = AR.alloc([4], F32)
            pgs = AR.alloc([24], F32)
            junkb = AR.alloc([128], BF16)
            Bm4 = AR.alloc([4, 128], F32)
            Bm = [Bm4[:, q_, :] for q_ in range(4)]
            BTm4 = AR.alloc([4, 128], F32)
            BTm = [BTm4[:, q_, :] for q_ in range(4)]
            PTm4 = AR.alloc([4, 128], F32)
            PTm = [PTm4[:, q_, :] for q_ in range(4)]
            TTb4 = AR.alloc([4, 128], BF16)
            TTb = [TTb4[:, q_, :] for q_ in range(4)]
            kw = [AR.alloc([128], BF16) for _ in range(4)]
            kd = [AR.alloc([128], BF16) for _ in range(4)]
            vb = [AR.alloc([128], BF16) for _ in range(4)]
            wT4 = AR.alloc([4, 128], BF16)
            wT = [wT4[:, q_, :] for q_ in range(4)]
            vn4 = AR.alloc([4, 128], BF16)
            vn = [vn4[:, q_, :] for q_ in range(4)]
            QgT4 = AR.alloc([4, 128], BF16)
            QgT = [QgT4[:, q_, :] for q_ in range(4)]
            qkmT4 = AR.alloc([4, 128], BF16)
            qkmT = [qkmT4[:, q_, :] for q_ in range(4)]
            Sbf4 = AR.alloc([4, 128], BF16)
            Sbf = [Sbf4[:, q_, :] for q_ in range(4)]
            usb4 = AR.alloc([4, 128], F32)
            usb = [usb4[:, q_, :] for q_ in range(4)]
            Sst4 = AR.alloc([4, 128], F32)
            Sst = [Sst4[:, q_, :] for q_ in range(4)]
            pT = [[AR.alloc([128], BF16) for _ in range(4)] for _ in range(2)]
            Gb, DA4, DA, DT4, DT, eg4, eg = Bm, PTm4, PTm, usb4, usb, BTm4, BTm
            print("phase A arena words used", AR.off, "of", NW)

            for nm in ("gq", "gk", "gv", "fq", "fk", "gz", "fv", "gb", "ga", "ff"):
                s0, d0, wd = SRC[nm], DST[nm], WID[nm]
                S.op("pool", DMA(Win[:, :, d0:d0 + wd],
                                 win_d.rearrange("(k p) c -> p k c", p=128)[:, :, s0:s0 + wd]),
                     writes=["Win_" + nm], dma="T_w_" + nm)
            S.op("pool", DMA(Wout, wout_d.rearrange("(k p) c -> p k c", p=128)), writes=["Wout"], dma="T_wo")
            small_load(convw, convw_d.rearrange("p (a b) -> p a b", a=12))
            small_load(anw, anw_d)
            small_load(gnw, gnw_d.rearrange("p (a b) -> p a b", a=4))
            small_load(raw12, raw12_d)
            small_load(sgn12, sgn12_d)
            small_load(alog, alog_d)
            S.op("pool", MS(V[:, :, :, 64:65], 1.0), writes=["V%d" % t_ for t_ in range(NT)])
            S.op("pool", MS(pre, 0.0), writes=["pre"])
            S.op("pool", MS(accf, 0.0), writes=["accf"])
            S.op("pool", MS(CC, 0.0), writes=["CC"])
            for h in range(4):
                S.op("pool", MS(Sst[h], 0.0), writes=["S%d" % h])
                S.op("pool", MS(Sbf[h], 0.0), writes=["Sbf%d" % h])
            S.op("dve", TT(bias12, raw12, sgn12, ALU.mult), reads=["const"], writes=["g12"])
            S.op("act", ACTF(mul12[:, 0:4], alog, AF.Exp), reads=["const"], writes=["m12a"])
            S.op("dve", TS(mul12[:, 0:4], mul12[:, 0:4], -1.0, None, ALU.mult), reads=["m12a"], writes=["m12"])
            S.op("dve", MS(mul12[:, 4:12], -1.0), writes=["m12b"])

            def pf_gen(i):
                n = 16 if i == 0 else 128
                pos = 0 if i == 0 else 16 + 128 * (i - 1)
                p_ = i % 2
                xt, u, uT, QTf, zs = xtB[p_], uB[p_], uTB[p_], QTfB[p_], zsB[p_]
                XT, QTFN, ZSN = "xt%d" % p_, "QTf%d" % p_, "zs%d" % p_
                UC = ["uc%d_%d" % (p_, k) for k in range(8)]
                UT = ["uT%d_%d" % (p_, k) for k in range(8)]
                src = meta_d if i == 0 else x_d[(i - 1) * 128:i * 128, :]
                S.op("sp", DMA(xt[:n], src), writes=[XT], dma="xld%d" % p_)
                yield
                S.op("pool", MS(ssq[:, 0:1], 0.0), writes=["ssq"])
                S.op("act", ACTF(u[:n], xt[:n], AF.Square, accum_out=ssq[:n, 0:1]), reads=[XT], writes=UC + ["ssq"])
                S.op("dve", TS(ssq[:n, 1:2], ssq[:n, 0:1], 1.0 / D, EPS, ALU.mult, ALU.add), reads=["ssq"], writes=["ssq1"])
                S.op("act", ACTF(ssq[:n, 2:3], ssq[:n, 1:2], AF.Ln), reads=["ssq1"], writes=["ssq2"])
                S.op("act", ACTF(ssq[:n, 3:4], ssq[:n, 2:3], AF.Exp, scale=-0.5), reads=["ssq2"], writes=["rstd"])
                S.op("dve", TS(u[:n], xt[:n], ssq[:n, 3:4], None, ALU.mult), reads=[XT, "rstd"], writes=UC)
                yield
                for kc in range(8):
                    S.op("pe", TR(pb1b[:, kc * 128:kc * 128 + n], u[:n, kc * 128:(kc + 1) * 128], identb[:n, :n]),
                         reads=[UC[kc], "const"], writes=["P1"])
                e3 = "act" if i % 2 else "dve"
                for kc in range(8):
                    evac_copy(uT[:, kc, :n], pb1b[:, kc * 128:kc * 128 + n], ["P1", "const"], [UT[kc]],
                              scale=anw[:, kc:kc + 1], eng=e3)
                yield
                for g in range(5):
                    bi = 2 - g % 2
                    bank, bn = pb[bi], "P%d" % bi
                    for c4 in range(4):
                        c = 4 * g + c4
                        for kc in range(8):
                            S.op("pe", MM(bank[:, c4 * 128:c4 * 128 + n], Win[:, kc, c * 128:(c + 1) * 128],
                                          uT[:, kc, :n], start=(kc == 0), stop=(kc == 7)),
                                 reads=[UT[kc], "Win_" + ("gq", "gk", "gv", "fq", "fk")[g]], writes=[bn])
                    srcp = bank.rearrange("p (c t) -> p c t", c=4)[:, :, :n]
                    yield
                    if g < 3:
                        evac_copy(pre[:, 4 * g:4 * g + 4, 3:3 + n], srcp, [bn], ["pre"])
                    elif g == 3:
                        evac_copy(QTf[:, :, :n], srcp, [bn], [QTFN], scale=0.125)
                    else:
                        evac_copy(KT[:, :, pos:pos + n], srcp, [bn], ["KT%d" % i])
                yield
                if i > 0:
                    for kc in range(8):
                        S.op("pe", MM(pb[1][:n, :], uT[:, kc, :n], Win[:, kc, 2560:3072], start=(kc == 0), stop=(kc == 7)),
                             reads=[UT[kc], "Win_gz"], writes=["P1"])
                    S.op("act", ACTF(zs[:n].rearrange("p a b -> p (a b)"), pb[1][:n, :], AF.Silu), reads=["P1"], writes=[ZSN])
                    S.op("pool", TT(zs[:n], zs[:n], gnw[:n], ALU.mult), reads=[ZSN, "const"], writes=[ZSN])
                for kc in range(8):
                    S.op("pe", MM(pb[2][:n, :], uT[:, kc, :n], Win[:, kc, 3072:3584], start=(kc == 0), stop=(kc == 7)),
                         reads=[UT[kc], "Win_fv"], writes=["P2"])
                evac_copy(V[:n, i, :, 0:64], pb[2][:n, :].rearrange("p (h d) -> p h d", h=8), ["P2"], ["V%d" % i])
                for kc in range(8):
                    S.op("pe", MM(pb[1][:n, 0:16], uT[:, kc, :n], Win[:, kc, 3584:3600], start=(kc == 0), stop=(kc == 7)),
                         reads=[UT[kc], "Win_gb", "Win_ga", "Win_ff"], writes=["P1"])
                S.op("dve", CP(sm[:n], pb[1][:n, 0:16]), reads=["P1"], writes=["sm"])

                yield
                for c in range(12):
                    yield
                    e = "dve"
                    S.op(e, TS(cv[:, c, :n], pre[:, c, 3:3 + n], convw[:, c, 3:4], None, ALU.mult),
                         reads=["pre", "const"], writes=["cv%d" % c])
                    for k in range(3):
                        if e == "dve":
                            S.op(e, STT(cv[:, c, :n], pre[:, c, k:k + n], convw[:, c, k:k + 1], cv[:, c, :n], ALU.mult, ALU.add),
                                 reads=["pre", "const", "cv%d" % c], writes=["cv%d" % c])
                        else:
                            S.op(e, TS(ctmp[:, :n], pre[:, c, k:k + n], convw[:, c, k:k + 1], None, ALU.mult),
                                 reads=["pre", "const"], writes=["ctmp"])
                            S.op(e, TT(cv[:, c, :n], cv[:, c, :n], ctmp[:, :n], ALU.add),
                                 reads=["ctmp", "cv%d" % c], writes=["cv%d" % c])
                cvall = ["cv%d" % c for c in range(12)]
                S.op("dve", CP(pre[:, :, 0:3], pre[:, :, n:n + 3]), reads=["pre"], writes=["pre"])
                S.op("act", ACTF(cv[:, 0:8, :n], cv[:, 0:8, :n], AF.Silu), reads=cvall, writes=cvall + ["cvs"])
                S.op("act", ACTF(VTg[:, :, :n], cv[:, 8:12, :n], AF.Silu), reads=cvall, writes=["VTg"])
                sq = pre[:, 0:8, 3:3 + n]
                S.op("dve", TT(sq, cv[:, 0:8, :n], cv[:, 0:8, :n], ALU.mult), reads=["cvs"], writes=["pre"])
                for _ in range(6):
                    yield
                for g2 in range(2):
                    bank, bn = pb[1 + g2], "P%d" % (1 + g2)
                    for c4 in range(4):
                        S.op("pe", MM(bank[:, c4 * 128:c4 * 128 + n], ONES, pre[:, 4 * g2 + c4, 3:3 + n]),
                             reads=["pre", "const"], writes=[bn])
                yield
                for g2 in range(2):
                    bank, bn = pb[1 + g2], "P%d" % (1 + g2)
                    srcp = bank.rearrange("p (c t) -> p c t", c=4)[:, :, :n]
                    dst = pre[:, 4 * g2:4 * g2 + 4, 3:3 + n]
                    S.op("dve", TS(dst, srcp, EPS, None, ALU.add), reads=[bn], writes=["pre"])
                    S.op("act", ACTF(dst, dst, AF.Ln), reads=["pre"], writes=["pre"])
                    S.op("act", ACTF(dst, dst, AF.Exp, scale=-0.5), reads=["pre"], writes=["pre"])
                S.op("dve", STT(QTg[:, :, :n], cv[:, 0:4, :n], 128.0 ** -0.5, pre[:, 0:4, 3:3 + n], ALU.mult, ALU.mult),
                     reads=["cvs", "pre"], writes=["QTg"])
                S.op("dve", TT(KTg[:, :, :n], cv[:, 4:8, :n], pre[:, 4:8, 3:3 + n], ALU.mult), reads=["cvs", "pre"], writes=["KTg"])

            WIN_NAMES = ["Win_" + nm for nm in ("gq", "gk", "gv", "fq", "fk", "gz", "fv", "gb", "ga", "ff")]
            Wg_early = Win.rearrange("p k c -> p (k c)")[:, 0:8 * DFF].rearrange("p (k c) -> p k c", k=8)

            def tile_body(i):
                n = 16 if i == 0 else 128
                pos = 0 if i == 0 else 16 + 128 * (i - 1)
                if i == NT - 1 and "B" in phases and stop is None:
                    S.op("pool", DMA(Wg_early, wg_d.rearrange("(k p) c -> p k c", p=128)), writes=WIN_NAMES + ["Wg"], dma="T_w2")
                p_ = i % 2
                xt, u, uT, QTf, zs = xtB[p_], uB[p_], uTB[p_], QTfB[p_], zsB[p_]
                XT, QTFN, ZSN = "xt%d" % p_, "QTf%d" % p_, "zs%d" % p_
                UC = ["uc%d_%d" % (p_, k) for k in range(8)]
                UT = ["uT%d_%d" % (p_, k) for k in range(8)]
                S.op("dve", TT(gt[:n, 0, :], sm[:n, 4:16], sgn12[:n], ALU.mult), reads=["sm", "const"], writes=["gt0"])
                S.op("dve", TT(gt[:n, 0, :], gt[:n, 0, :], bias12[:n], ALU.add), reads=["gt0", "g12"], writes=["gt0"])
                S.op("act", ACTF(gt[:n, 1, :], gt[:n, 0, :], AF.Exp), reads=["gt0"], writes=["gt1"])
                S.op("dve", TS(gt[:n, 1, :], gt[:n, 1, :], 1.0, None, ALU.add), reads=["gt1"], writes=["gt1"])
                S.op("act", ACTF(gt[:n, 2, :], gt[:n, 1, :], AF.Ln), reads=["gt1"], writes=["gt2"])
                S.op("dve", TT(gt[:n, 3, :], gt[:n, 2, :], mul12[:n], ALU.mult), reads=["gt2", "m12", "m12b"], writes=["GL"])
                gcol = gt[:, 3, 0:4]
                lfcol = gt[:, 3, 4:12]
                S.op("act", ACTF(sm4[:n, 0, :], sm[:n, 0:4], AF.Exp, scale=-1.0), reads=["sm"], writes=["b0"])
                S.op("dve", TS(sm4[:n, 0, :], sm4[:n, 0, :], 1.0, None, ALU.add), reads=["b0"], writes=["b0"])
                S.op("dve", RCP(sm4[:n, 1, :], sm4[:n, 0, :]), reads=["b0"], writes=["beta"])
                S.op("dve", TS(sm4[:n, 2, :], sm4[:n, 1, :], -1.0, None, ALU.mult), reads=["beta"], writes=["nbeta"])
                beta, nbeta, ngc, egc, bege, egl, ekd, gtmp = (sm4[:, k, :] for k in range(1, 9))
                pg = pb[3]
                S.op("pe", MM(pg[:n, 0:4], TRI[:n, :n], gcol[:n]), reads=["GL", "const"], writes=["P3"])
                S.op("pe", MM(pg[:, 4:8], ONES[:n, :], gcol[:n]), reads=["GL", "const"], writes=["P3"])
                S.op("pe", MM(pg[:n, 8:16], TRI[:n, :n], lfcol[:n], start=True, stop=False), reads=["GL", "const"], writes=["P3"])
                S.op("pe", MM(pg[:n, 8:16], ONES[:, :n], accf, start=False, stop=True), reads=["accf", "const"], writes=["P3"])
                hsel = ONES if i == 0 else HALF
                S.op("pe", MM(pg[:, 16:24], hsel[:n, :], lfcol[:n], start=True, stop=False), reads=["GL", "const"], writes=["P3"])
                S.op("pe", MM(pg[:, 16:24], ONES, accf, start=False, stop=True), reads=["accf", "const"], writes=["P3"])
                S.op("dve", CP(pgs[:, 4:8], pg[:, 4:8]), reads=["P3"], writes=["pgs"])
                S.op("dve", CP(pgs[:, 16:24], pg[:, 16:24]), reads=["P3"], writes=["pgs"])
                S.op("dve", CP(pgs[:n, 0:4], pg[:n, 0:4]), reads=["P3"], writes=["pgs"])
                S.op("dve", CP(pgs[:n, 8:16], pg[:n, 8:16]), reads=["P3"], writes=["pgs"])
                S.op("dve", TS(ngc[:n], pgs[:n, 0:4], -1.0, None, ALU.mult), reads=["pgs"], writes=["ngc"])
                S.op("act", ACTF(egc[:n], pgs[:n, 0:4], AF.Exp), reads=["pgs"], writes=["egc"])
                S.op("dve", TT(bege[:n], beta[:n], egc[:n], ALU.mult), reads=["beta", "egc"], writes=["bege"])
                S.op("act", ACTF(egl, pgs[:, 4:8], AF.Exp), reads=["pgs"], writes=["egl"])
                S.op("dve", TT(gtmp[:n], pgs[:n, 4:8], ngc[:n], ALU.add), reads=["pgs", "ngc"], writes=["gtmp"])
                S.op("act", ACTF(ekd[:n], gtmp[:n], AF.Exp), reads=["gtmp"], writes=["ekd"])
                S.op("pool", CP(CC[:n, i, :], pgs[:n, 8:16]), reads=["pgs"], writes=["CC"])
                S.op("pool", CP(cref, pgs[:, 16:24]), reads=["pgs"], writes=["cref"])
                S.op("pool", TT(accf[:n], accf[:n], lfcol[:n], ALU.add), reads=["GL", "accf"], writes=["accf"])
                if i > 0:
                    for h in range(8):
                        S.op("pool", TS(biasT[:, 0:i + 1, h], CC[:, 0:i + 1, h], -1.0, cref[:, h:h + 1], ALU.mult, ALU.add),
                             reads=["CC", "cref"], writes=["biasT"])
                checkpoint(i, "gates", locals())
                def gdn_gen():
                    yield
                    for h in range(4):
                        S.op("pe", MM(pb[1][:n, h * 128:(h + 1) * 128], KTg[:, h, :n], identb), reads=["KTg", "const"], writes=["P1"])
                    yield
                    for h in range(4):
                        S.op("pe", MM(pb[2][:n, h * 128:(h + 1) * 128], VTg[:, h, :n], identb), reads=["VTg", "const"], writes=["P2"])
                    yield
                    for h in range(4):
                        S.op("dve", TS(kw[h][:n], pb[1][:n, h * 128:(h + 1) * 128], bege[:n, h:h + 1], None, ALU.mult),
                             reads=["P1", "bege"], writes=["kw%d" % h])
                        S.op("dve", TS(kd[h][:n], pb[1][:n, h * 128:(h + 1) * 128], ekd[:n, h:h + 1], None, ALU.mult),
                             reads=["P1", "ekd"], writes=["kd%d" % h])
                        S.op("act", AMUL(vb[h][:n], pb[2][:n, h * 128:(h + 1) * 128], beta[:n, h:h + 1]),
                             reads=["P2", "beta"], writes=["vb%d" % h])
                    checkpoint(i, "prep", locals())
                    yield "PREP_DONE"
                    yield
                    for h in range(4):
                        S.op("dve", TS(Gb[h][:n], ONES[:n], gcol[:n, h:h + 1], None, ALU.mult), reads=["GL", "const"], writes=["B%d" % h])
                        S.op("pe", MM(pb[4][:, h * 128:h * 128 + n], Gb[h][:n], TRI[:n, :n]), reads=["B%d" % h, "const"], writes=["P4"])
                    yield
                    for h in range(4):
                        S.op("pe", MM(pb[5][:n, h * 128:h * 128 + n], KTg[:, h, :n], KTg[:, h, :n]), reads=["KTg"], writes=["P5"])
                    if i > 0:
                        for h in range(4):
                            S.op("pe", MM(pb[6][:n, h * 128:h * 128 + n], KTg[:, h, :n], QTg[:, h, :n]), reads=["KTg", "QTg"], writes=["P6"])
                    yield
                    S.op("dve", CP(eg4[:, :, :n], pb[4].rearrange("p (h c) -> p h c", h=4)[:, :, :n]), reads=["P4"], writes=["BT%d" % q for q in range(4)])
                    yield
                    for h in range(4):
                        gr = eg[h][:n, :n]
                        S.op("dve", STT(DA[h][:n, :n], gr, ngc[:n, h:h + 1], POSA[:n, :n], ALU.add, ALU.add),
                             reads=["BT%d" % h, "ngc", "const"], writes=["PT%d" % h])
                    S.op("act", ACTF(DA4[:n, :, :n], DA4[:n, :, :n], AF.Exp, scale=-1.0), reads=["PT%d" % q for q in range(4)], writes=["PT%d" % q for q in range(4)])
                    for h in range(4):
                        gr = eg[h][:n, :n]
                        S.op("dve", STT(DT[h][:n, :n], gr, ngc[:n, h:h + 1], NEGD[:n, :n], ALU.add, ALU.add),
                             reads=["BT%d" % h, "ngc", "const"], writes=["usb%d" % h])
                    S.op("act", ACTF(DT4[:n, :, :n], DT4[:n, :, :n], AF.Exp), reads=["usb%d" % q for q in range(4)], writes=["usb%d" % q for q in range(4)])
                    if i > 0:
                        S.op("act", ACTF(eg4[:, :, :n], eg4[:, :, :n], AF.Exp), reads=["BT%d" % q for q in range(4)], writes=["BT%d" % q for q in range(4)])
                        S.op("pool", TT(QgT4[:, :, :n], QTg[:, :, :n], eg4[:, :, :n], ALU.mult), reads=["QTg"] + ["BT%d" % q for q in range(4)],
                             writes=["QgT%d" % q for q in range(4)])
                    yield
                    for h in range(4):
                        S.op("dve", STT(Bm[h][:n, :n], pb[5][:n, h * 128:h * 128 + n], nbeta[:n, h:h + 1], DA[h][:n, :n], ALU.mult, ALU.mult),
                             reads=["P5", "nbeta", "PT%d" % h], writes=["B%d" % h])
                    if i > 0:
                        S.op("dve", TT(qkmT4[:n, :, :n], pb[6].rearrange("p (h c) -> p h c", h=4)[:n, :, :n], DT4[:n, :, :n], ALU.mult),
                             reads=["P6"] + ["usb%d" % q for q in range(4)], writes=["qkmT%d" % q for q in range(4)])
                    yield
                    for h in range(4):
                        S.op("pe", TR(pb[4][:n, h * 128:h * 128 + n], Bm[h][:n, :n], identf[:n, :n]), reads=["B%d" % h, "const"], writes=["P4"])
                    yield
                    S.op("dve", CP(BTm4[:n, :, :n], pb[4].rearrange("p (h c) -> p h c", h=4)[:n, :, :n]), reads=["P4"], writes=["BT%d" % q for q in range(4)])
                    for h in range(4):
                        S.op("dve", TT(PTm[h][:n, :n], BTm[h][:n, :n], identf[:n, :n], ALU.add), reads=["BT%d" % h, "const"], writes=["PT%d" % h])
                    checkpoint(i, "gdn7a", locals())
                    nst = 6 if i > 0 else 3
                    yield
                    for k in range(nst):
                        last = (k == nst - 1)
                        yield
                        for h in range(4):
                            S.op("pe", MM(pb[4][:n, h * 128:h * 128 + n], BTm[h][:n, :n], Bm[h][:n, :n]),
                                 reads=["BT%d" % h, "B%d" % h], writes=["P4"])
                        if not last:
                            for h in range(4):
                                S.op("pe", MM(pb[5][:n, h * 128:h * 128 + n], Bm[h][:n, :n], BTm[h][:n, :n]),
                                     reads=["BT%d" % h, "B%d" % h], writes=["P5"])
                        yield
                        for h in range(4):
                            S.op("dve", CP(Bm[h][:n, :n], pb[4][:n, h * 128:h * 128 + n]), reads=["P4"], writes=["B%d" % h])
                        if not last:
                            S.op("act", ACTF(BTm4[:n, :, :n], pb[5].rearrange("p (h c) -> p h c", h=4)[:n, :, :n], AF.Copy), reads=["P5"], writes=["BT%d" % q for q in range(4)])
                        yield
                        for h in range(4):
                            S.op("pe", MM(pb[6][:n, h * 128:h * 128 + n], Bm[h][:n, :n], PTm[h][:n, :n]),
                                 reads=["B%d" % h, "PT%d" % h], writes=["P6"])
                        yield
                        for h in range(4):
                            S.op("dve", TT(PTm[h][:n, :n], pb[6][:n, h * 128:h * 128 + n], PTm[h][:n, :n], ALU.add),
                                 reads=["P6", "PT%d" % h], writes=["PT%d" % h])
                    yield
                    S.op("dve", CP(TTb4[:n, :, :n], PTm4[:n, :, :n]), reads=["PT%d" % q for q in range(4)], writes=["TTb%d" % q for q in range(4)])
                    yield
                    for h in range(4):
                        S.op("pe", MM(pb[4][:n, h * 128:(h + 1) * 128], TTb[h][:n, :n], vb[h][:n, :]), reads=["TTb%d" % h, "vb%d" % h], writes=["P4"])
                    yield
                    for h in range(4):
                        S.op("pe", MM(pb[5][:, h * 128:h * 128 + n], kw[h][:n, :], TTb[h][:n, :n]), reads=["TTb%d" % h, "kw%d" % h], writes=["P5"])
                    yield
                    S.op("dve", CP(usb4[:n], pb[4].rearrange("p (h c) -> p h c", h=4)[:n]), reads=["P4"], writes=["usb%d" % q for q in range(4)])
                    yield
                    S.op("dve", CP(wT4[:, :, :n], pb[5].rearrange("p (h c) -> p h c", h=4)[:, :, :n]), reads=["P5"], writes=["wT%d" % q for q in range(4)])
                    checkpoint(i, "gdn7", locals())
                    yield
                    for h in range(4):
                        S.op("pe", MM(pb[6][:n, h * 128:(h + 1) * 128], wT[h][:, :n], Sbf[h]), reads=["wT%d" % h, "Sbf%d" % h], writes=["P6"])
                    yield
                    S.op("dve", TT(vn4[:n], usb4[:n], pb[6].rearrange("p (h c) -> p h c", h=4)[:n], ALU.subtract),
                         reads=["P6"] + ["usb%d" % q for q in range(4)], writes=["vn%d" % q for q in range(4)])
                    if i > 0:
                        for h in range(4):
                            S.op("pe", MM(pb[4][:n, h * 128:(h + 1) * 128], QgT[h][:, :n], Sbf[h], start=True, stop=False),
                                 reads=["QgT%d" % h, "Sbf%d" % h], writes=["P4"])
                            S.op("pe", MM(pb[4][:n, h * 128:(h + 1) * 128], qkmT[h][:n, :n], vn[h][:n, :], start=False, stop=True),
                                 reads=["qkmT%d" % h, "vn%d" % h], writes=["P4"])
                    yield
                    for h in range(4):
                        S.op("pe", MM(pb[5][:, h * 128:(h + 1) * 128], kd[h][:n, :], vn[h][:n, :]), reads=["kd%d" % h, "vn%d" % h], writes=["P5"])
                    if i > 0:
                        S.op("pool", MS(so[:, 0:4], 0.0), writes=["so"])
                        for h in range(4):
                            S.op("act", ACTF(junkb[:n, :], pb[4][:n, h * 128:(h + 1) * 128], AF.Square, accum_out=so[:n, h:h + 1]),
                                 reads=["P4"], writes=["so", "junkb"])
                        S.op("dve", TS(so[:n, 4:8], so[:n, 0:4], 1.0 / 128, EPS, ALU.mult, ALU.add), reads=["so"], writes=["so"])
                        S.op("act", ACTF(so[:n, 4:8], so[:n, 4:8], AF.Ln), reads=["so"], writes=["so"])
                        S.op("act", ACTF(so[:n, 4:8], so[:n, 4:8], AF.Exp, scale=-0.5), reads=["so"], writes=["so"])
                        for h in range(4):
                            S.op("act", AMUL(usb[h][:n, :], pb[4][:n, h * 128:(h + 1) * 128], so[:n, 4 + h:5 + h]),
                                 reads=["P4", "so"], writes=["usb%d" % h])
                    if i > 0:
                        S.op("dve", TT(u[:n, 0:512].rearrange("p (h c) -> p h c", h=4), usb4[:n], zs[:n], ALU.mult),
                             reads=["usb%d" % q for q in range(4)] + [ZSN], writes=UC[0:4])
                    yield
                    for h in range(4):
                        S.op("dve", STT(Sst[h], Sst[h], egl[:, h:h + 1], pb[5][:, h * 128:(h + 1) * 128], ALU.mult, ALU.add),
                             reads=["P5", "egl", "S%d" % h], writes=["S%d" % h])
                    S.op("pool", CP(Sbf4, Sst4), reads=["S%d" % q for q in range(4)], writes=["Sbf%d" % q for q in range(4)])
                    yield
                def attn_gen():
                    items = []
                    for h in range(8):
                        kts = list(range(i + 1))
                        for c0 in range(0, len(kts), 4):
                            items.append((h, kts[c0:c0 + 4]))

                    def scores(idx):
                        h, ks = items[idx]
                        hp, r0 = h // 2, 64 * (h % 2)
                        bi = (0, 7)[idx % 2]
                        st_ = idx % 2
                        for j, kt in enumerate(ks):
                            nk = 16 if kt == 0 else 128
                            kpos = 0 if kt == 0 else 16 + 128 * (kt - 1)
                            S.op("pe", MM(pb[bi][:nk, j * 128:(j + 1) * 128], KT[r0:r0 + 64, hp, kpos:kpos + nk], QTf[r0:r0 + 64, hp, :]),
                                 reads=["KT%d" % kt, QTFN], writes=["P%d" % bi])
                        for j, kt in enumerate(ks):
                            nk = 16 if kt == 0 else 128
                            S.op("act", ACTF(pT[st_][j][:nk, :], pb[bi][:nk, j * 128:(j + 1) * 128], AF.Exp, bias=biasT[:nk, kt, h:h + 1], scale=1.0),
                                 reads=["P%d" % bi, "biasT"], writes=["pT%d_%d" % (st_, j)])
                            if kt == i:
                                S.op("dve", TT(pT[st_][j][:nk, :], pT[st_][j][:nk, :], MASK01[:nk, :], ALU.mult),
                                     reads=["pT%d_%d" % (st_, j), "const"], writes=["pT%d_%d" % (st_, j)])

                    def pv(idx):
                        h, ks = items[idx]
                        st_ = idx % 2
                        hh = h % 4
                        fb = 3
                        for j, kt in enumerate(ks):
                            nk = 16 if kt == 0 else 128
                            S.op("pe", MM(pb[fb][:, hh * 65:(hh + 1) * 65], pT[st_][j][:nk, :], V[:nk, kt, h, :], start=(kt == 0), stop=(kt == i)),
                                 reads=["pT%d_%d" % (st_, j), "V%d" % kt], writes=["P%d" % fb])
                        if ks[-1] == i and hh == 3:
                            fo3 = pb[fb][:, 0:260].rearrange("p (h d) -> p h d", h=4)
                            S.op("dve", RCP(rd, fo3[:, :, 64]), reads=["P%d" % fb], writes=["rd"])
                            for q in range(4):
                                hq = h - 3 + q
                                evac_copy(u[:, 512 + hq * 64:512 + (hq + 1) * 64], fo3[:, q, 0:64], ["P%d" % fb, "rd"], [UC[4 + hq // 2]],
                                          scale=rd[:, q:q + 1], eng="dve")

                    scores(0)
                    for idx in range(len(items)):
                        if idx + 1 < len(items):
                            scores(idx + 1)
                        pv(idx)
                        yield
                def run_streams(gens, late=None, first=None):
                    gens = list(gens)
                    prep_done = False
                    while gens:
                        for g_ in list(gens):
                            try:
                                v_ = next(g_)
                            except StopIteration:
                                gens.remove(g_)
                                if g_ is first:
                                    first = None
                                continue
                            if v_ == "PREP_DONE":
                                prep_done = True
                        if late is not None and prep_done and first is None:
                            gens.append(late)
                            late = None
                    if late is not None:
                        for _ in late:
                            pass
                nxt = pf_gen(i + 1) if i + 1 < NT else None
                if i == 0:
                    run_streams([gdn_gen()], late=nxt)
                    return
                def out_gen():
                    for kc in range(8):
                        S.op("pe", TR(pb0b[:, kc * 128:(kc + 1) * 128], u[:, kc * 128:(kc + 1) * 128], identb), reads=[UC[kc], "const"], writes=["P0"])
                    for kc in range(8):
                        evac_copy(uT[:, kc, :], pb0b[:, kc * 128:(kc + 1) * 128], ["P0"], [UT[kc]], eng=("dve" if i % 2 else "act"))
                    yield
                    for half in range(2):
                        bi_ = (7, 0)[half]
                        bank, bn = pb[bi_], "P%d" % bi_
                        for kc in range(8):
                            S.op("pe", MM(bank, uT[:, kc, :], Wout[:, kc, half * 512:(half + 1) * 512], start=(kc == 0), stop=(kc == 7)),
                                 reads=[UT[kc], "Wout"], writes=[bn])
                        S.op("dve", TT(xt[:, half * 512:(half + 1) * 512], xt[:, half * 512:(half + 1) * 512], bank, ALU.add),
                             reads=[bn, XT], writes=[XT])
                        if half == 0:
                            yield
                    S.op("sp", DMA(h1_d[(i - 1) * 128:i * 128, :], xt), reads=[XT], writes=["h1d"], dma="h1st%d" % p_)

                prev_out = pending_out[0]
                pending_out[0] = out_gen()
                run_streams(([prev_out] if prev_out is not None else []) + [gdn_gen(), attn_gen()], late=nxt, first=prev_out)

            pending_out = [None]
            try:
                for _ in pf_gen(0):
                    pass
                for i in range(NT):
                    tile_body(i)
                if pending_out[0] is not None:
                    for _ in pending_out[0]:
                        pass
            except StopBuild:
                pass

        if "B" in phases and stop is None:
            S.barrier()
            AR.off = common_mark
            Wg = AR.alloc([8, DFF], BF16)
            Wu = AR.alloc([8, DFF], BF16)
            Wd = AR.alloc([NFC, D], BF16)
            fnw = AR.alloc([8], F32)
            finw = AR.alloc([D], F32)
            hb = AR.alloc([2, D], F32)
            hb2 = AR.alloc([2, D], F32)
            u2T2 = AR.alloc([8, 256], BF16)
            junk2 = AR.alloc([D], BF16)
            u2 = AR.alloc([D], BF16)
            u2T = AR.alloc([8, 256], BF16)
            actT = AR.alloc([NFC, 256], BF16)
            sg = [AR.alloc([256], F32) for _ in range(2)]
            st = AR.alloc([8], F32)
            print("phase B arena words used", AR.off, "of", NW)
            if "A" not in phases:
                S.op("pool", DMA(Wg, wg_d.rearrange("(k p) c -> p k c", p=128)), writes=["Wg"], dma="T_w2")
            S.op("pool", DMA(Wu, wu_d.rearrange("(k p) c -> p k c", p=128)), writes=["Wu"], dma="T_w2")
            S.op("pool", DMA(Wd, wd_d.rearrange("(k p) c -> p k c", p=128)), writes=["Wd"], dma="T_w2")
            S.op("sp", DMA(fnw, fnw_d), writes=["constB"], dma="T_c2")
            S.op("sp", DMA(finw, finw_d), writes=["constB"], dma="T_c2")
            NG = NR // 2
            hbB = [hb, hb2]
            u2TB = [u2T, u2T2]

            def prepB(g):
                q_ = g % 2
                hbq, u2Tq = hbB[q_], u2TB[q_]
                S.op("sp", DMA(hbq, h1_d[g * 256:(g + 1) * 256, :].rearrange("(t p) d -> p t d", p=128)),
                     reads=["h1d"], writes=["hb%d" % q_], dma="hld%d" % q_)
                for j in range(2):
                    S.op("pool", MS(st[:, 0:1], 0.0), writes=["st"])
                    S.op("act", ACTF(u2, hbq[:, j, :], AF.Square, accum_out=st[:, 0:1]), reads=["hb%d" % q_], writes=["u2", "st"])
                    S.op("dve", TS(st[:, 1:2], st[:, 0:1], 1.0 / D, EPS, ALU.mult, ALU.add), reads=["st"], writes=["st"])
                    S.op("act", ACTF(st[:, 2:3], st[:, 1:2], AF.Ln), reads=["st"], writes=["st"])
                    S.op("act", ACTF(st[:, 3:4], st[:, 2:3], AF.Exp, scale=-0.5), reads=["st"], writes=["st"])
                    S.op("dve", TS(u2, hbq[:, j, :], st[:, 3:4], None, ALU.mult), reads=["hb%d" % q_, "st"], writes=["u2"])
                    for kc in range(8):
                        S.op("pe", TR(pb0b[:, kc * 128:(kc + 1) * 128], u2[:, kc * 128:(kc + 1) * 128], identb),
                             reads=["u2", "const"], writes=["P0"])
                    for kc in range(8):
                        evac_copy(u2Tq[:, kc, j * 128:(j + 1) * 128], pb0b[:, kc * 128:(kc + 1) * 128], ["P0", "constB"],
                                  ["u2T%d_%d" % (q_, kc)], scale=fnw[:, kc:kc + 1], eng=("act" if j else "dve"))

            prepB(0)
            for g in range(NG):
                q_ = g % 2
                hbq, u2Tq = hbB[q_], u2TB[q_]
                U2T = ["u2T%d_%d" % (q_, k) for k in range(8)]
                for fc in range(NFC):
                    p2 = fc % 2
                    bg, bu = 3 + p2, 5 + p2
                    for kc in range(8):
                        S.op("pe", MM(pb[bg][:, 0:256], Wg[:, kc, fc * 128:(fc + 1) * 128], u2Tq[:, kc, :], start=(kc == 0), stop=(kc == 7)),
                             reads=["Wg", U2T[kc]], writes=["P%d" % bg])
                    for kc in range(8):
                        S.op("pe", MM(pb[bu][:, 0:256], Wu[:, kc, fc * 128:(fc + 1) * 128], u2Tq[:, kc, :], start=(kc == 0), stop=(kc == 7)),
                             reads=["Wu", U2T[kc]], writes=["P%d" % bu])
                    S.op("act", ACTF(sg[p2], pb[bg][:, 0:256], AF.Silu), reads=["P%d" % bg], writes=["sg%d" % p2])
                    S.op("dve", TT(actT[:, fc, :], sg[p2], pb[bu][:, 0:256], ALU.mult), reads=["sg%d" % p2, "P%d" % bu], writes=["actT%d" % fc])
                if g + 1 < NG:
                    prepB(g + 1)
                AT = ["actT%d" % fc for fc in range(NFC)]
                for j in range(2):
                    for half in range(2):
                        q = 1 + (2 * j + half) % 2
                        for fc in range(NFC):
                            S.op("pe", MM(pb[q], actT[:, fc, j * 128:(j + 1) * 128], Wd[:, fc, half * 512:(half + 1) * 512],
                                          start=(fc == 0), stop=(fc == NFC - 1)), reads=[AT[fc], "Wd"], writes=["P%d" % q])
                        sl = hbq[:, j, half * 512:(half + 1) * 512]
                        S.op("dve", TT(sl, sl, pb[q], ALU.add), reads=["P%d" % q, "hb%d" % q_], writes=["hb%d" % q_])
                    S.op("pool", MS(st[:, 4:5], 0.0), writes=["st2"])
                    S.op("act", ACTF(junk2, hbq[:, j, :], AF.Square, accum_out=st[:, 4:5]), reads=["hb%d" % q_], writes=["junk2", "st2"])
                    S.op("dve", TS(st[:, 5:6], st[:, 4:5], 1.0 / D, EPS, ALU.mult, ALU.add), reads=["st2"], writes=["st2"])
                    S.op("act", ACTF(st[:, 6:7], st[:, 5:6], AF.Ln), reads=["st2"], writes=["st2"])
                    S.op("act", ACTF(st[:, 7:8], st[:, 6:7], AF.Exp, scale=-0.5), reads=["st2"], writes=["st2"])
                    S.op("dve", STT(hbq[:, j, :], hbq[:, j, :], st[:, 7:8], finw, ALU.mult, ALU.mult), reads=["hb%d" % q_, "st2", "constB"], writes=["hb%d" % q_])
                S.op("sp", DMA(out_d[g * 256:(g + 1) * 256, :].rearrange("(t p) d -> p t d", p=128), hbq),
                     reads=["hb%d" % q_], writes=["outd"], dma="ost%d" % q_)

        fw = []
        for nm_ in ("ost0", "ost1", "h1st0", "h1st1", "T_dump"):
            if nm_ in S.dsem:
                fw.append(nm_)
        with nc.Block() as block:
            S.emit(block, final_waits=fw)
    return nc


def make_consts():
    p = np.arange(128)[:, None]
    f = np.arange(128)[None, :]
    identf = (p == f).astype(np.float32)
    tri = (p <= f).astype(np.float32)
    ones = np.ones((128, 128), np.float32)
    half = np.broadcast_to((p < 64), (128, 128)).astype(np.float32)
    posa = np.where(f < p, 0.0, 30000.0).astype(np.float32)
    negd = np.where(p <= f, 0.0, -30000.0).astype(np.float32)
    spare = np.zeros((128, 128), np.float32)
    cf = np.concatenate([identf, tri, ones, half, posa, negd, spare], axis=1)
    identb = identf.astype(ml_dtypes.bfloat16)
    mask01 = (p <= f).astype(np.float32).astype(ml_dtypes.bfloat16)
    cb = np.concatenate([identb, mask01], axis=1)
    return np.ascontiguousarray(cf), np.ascontiguousarray(cb)


def make_in_maps(x, meta_tokens, attn_norm_w, w_in, conv_w, a_log, dt_bias, gdn_norm_w, fgate_b,
                 w_out, ffn_norm_w, w_gate, w_up, w_down, final_norm_w):
    f = lambda a: np.ascontiguousarray(np.asarray(a, dtype=np.float32))
    cf, cb = make_consts()
    convw = f(np.asarray(conv_w)[0].reshape(4, 12, 128).transpose(2, 1, 0).reshape(128, 48))
    anw = f(np.asarray(attn_norm_w)[0].reshape(8, 128).T)
    fnw = f(np.asarray(ffn_norm_w)[0].reshape(8, 128).T)
    finw = f(np.broadcast_to(np.asarray(final_norm_w)[None, :], (128, D)))
    gnw = f(np.broadcast_to(np.tile(np.asarray(gdn_norm_w)[0], 4)[None, :], (128, 512)))
    raw12 = f(np.broadcast_to(np.concatenate([np.asarray(dt_bias)[0], np.asarray(fgate_b)[0]])[None, :], (128, 12)))
    sgn12 = f(np.broadcast_to(np.array([1.0] * 4 + [-1.0] * 8, np.float32)[None, :], (128, 12)))
    alog = f(np.broadcast_to(np.asarray(a_log)[0][None, :], (128, 4)))
    shared = dict(meta=f(meta_tokens), w_in=f(np.asarray(w_in)[0]), w_out=f(np.asarray(w_out)[0]),
                  w_gate=f(np.asarray(w_gate)[0]), w_up=f(np.asarray(w_up)[0]), w_down=f(np.asarray(w_down)[0]),
                  cf32=cf, cbf16=cb, convw=convw, anw=anw, fnw=fnw, finw=finw, gnw=gnw, raw12=raw12, sgn12=sgn12, alog=alog)
    x = np.asarray(x, dtype=np.float32)
    maps = []
    for b in range(x.shape[0]):
        m = dict(shared)
        m["x"] = np.ascontiguousarray(x[b])
        maps.append(m)
    return maps


def kernel(**inputs):
    x = np.asarray(inputs["x"])
    B, L, _ = x.shape
    NR = L // 128
    maps = make_in_maps(**inputs)
    nc = build_nc(NR)
    res = run_bass_kernel_spmd(nc, maps, core_ids=list(range(B)))
    out = np.stack([np.asarray(res.results[b]["out"]).reshape(L, D) for b in range(B)], axis=0)
    return out.astype(np.float32)
```

```python
import numpy as np
import ml_dtypes
from contextlib import ExitStack
import concourse.bass as bass
import concourse.mybir as mybir
from concourse.bass_utils import run_bass_kernel_spmd

F32 = mybir.dt.float32
BF16 = mybir.dt.bfloat16
AF = mybir.ActivationFunctionType
ALU = mybir.AluOpType

D = 1024
DIN = 3600
DFF = 2816
NFC = DFF // 128
EPS = 1e-6
NCORES = 8
SEQ = 4096


class Sched:
    ENG = ("pe", "act", "dve", "pool", "sp")

    def __init__(self, nc, es):
        self.nc = nc
        self.es = es
        self.ops = {e: [] for e in self.ENG}
        self.last_w = {}
        self.readers = {}
        self.sems = {e: es.enter_context(nc.semaphore("s_" + e)) for e in self.ENG}
        self.dsem = {}
        self.dcount = {}
        self.dlast = {}
        self.barrier_deps = []

    def op(self, eng, fn, reads=(), writes=(), dma=None):
        o = dict(eng=eng, fn=fn, deps=[], sig=False, dma=dma)
        for w in self.barrier_deps:
            self._dep(o, w)
        for r in reads:
            w = self.last_w.get(r)
            if w is not None:
                self._dep(o, w)
            if len(r) == 2 and r[0] == "P" and r[1].isdigit():
                for rd in self.readers.get(r, ()):
                    if rd["eng"] != eng:
                        self._dep(o, rd)
        for r in writes:
            w = self.last_w.get(r)
            if w is not None:
                self._dep(o, w)
            for rd in self.readers.get(r, ()):
                self._dep(o, rd)
        for r in reads:
            self.readers.setdefault(r, []).append(o)
        for r in writes:
            self.last_w[r] = o
            self.readers[r] = []
        if dma is not None:
            if dma not in self.dsem:
                self.dsem[dma] = self.es.enter_context(self.nc.semaphore("d_" + dma))
                self.dcount[dma] = 0
            self.dcount[dma] += 1
            o["dval"] = 16 * self.dcount[dma]
            self.dlast[dma] = o
        self.ops[eng].append(o)
        return o

    def _dep(self, o, w):
        if w is o:
            return
        if w["dma"] is None and o["dma"] is None and w["eng"] == "pe" and o["eng"] == "pe":
            return
        if w["dma"] is not None and o["dma"] == w["dma"] and w["dma"].startswith("T_"):
            return
        o["deps"].append(w)
        if w["dma"] is None:
            w["sig"] = True

    def barrier(self):
        deps = []
        for e in self.ENG:
            if self.ops[e]:
                deps.append(self.ops[e][-1])
        for d in self.dlast.values():
            deps.append(d)
        for w in deps:
            if w["dma"] is None:
                w["sig"] = True
        self.barrier_deps = deps

    def emit(self, block, final_waits=()):
        for e in self.ENG:
            c = 0
            for o in self.ops[e]:
                if o["dma"] is None and o["sig"]:
                    c += 1
                    o["sval"] = c
        S = self

        def run(eng_name):
            def body(eng):
                waited = {}
                for o in S.ops[eng_name]:
                    need = {}
                    for w in o["deps"]:
                        if w["dma"] is not None:
                            nm = w["dma"]
                            sem = S.dsem[nm]
                            val = 16 * S.dcount[nm] if nm.startswith("T_") else w["dval"]
                        else:
                            sem = S.sems[w["eng"]]
                            val = w["sval"]
                        k = sem.num
                        if k not in need or need[k][1] < val:
                            need[k] = (sem, val)
                    for k, (sem, val) in need.items():
                        if waited.get(k, 0) >= val:
                            continue
                        eng.wait_ge(sem, val)
                        waited[k] = val
                    ins = o["fn"](eng)
                    if o["dma"] is not None:
                        ins.then_inc(S.dsem[o["dma"]], 16)
                    elif o["sig"]:
                        ins.then_inc(S.sems[eng_name], 1)
                if eng_name == "sp":
                    for nm in final_waits:
                        eng.wait_ge(S.dsem[nm], 16 * S.dcount[nm])
            return body

        block.tensor(run("pe"))
        block.scalar(run("act"))
        block.vector(run("dve"))
        block.gpsimd(run("pool"))
        block.sync(run("sp"))


class Arena:
    def __init__(self, handle, nwords):
        self.h = handle
        self.n = nwords
        self.off = 0

    def alloc(self, free_shape, dtype):
        n = int(np.prod(free_shape))
        if dtype == BF16:
            words = (n + 1) // 2
        else:
            words = n
        assert self.off + words <= self.n, ("arena overflow", self.off, words, self.n)
        ap = self.h[:, self.off:self.off + words]
        self.off += words
        if dtype == BF16:
            ap = ap.bitcast(BF16)[:, 0:n]
        if len(free_shape) == 2:
            ap = ap.rearrange("p (a b) -> p a b", a=free_shape[0])
        elif len(free_shape) == 3:
            ap = ap.rearrange("p (a b c) -> p a b c", a=free_shape[0], b=free_shape[1])
        return ap


def MM(out, lhsT, rhs, start=True, stop=True):
    return lambda e: e.matmul(out, lhsT=lhsT, rhs=rhs, start=start, stop=stop)


def TR(out, in_, ident):
    return lambda e: e.transpose(out, in_, ident)


def ACTF(out, in_, func, bias=None, scale=None, accum_out=None):
    kw = {}
    if bias is not None:
        kw["bias"] = bias
    if scale is not None:
        kw["scale"] = scale
    if accum_out is not None:
        kw["accum_out"] = accum_out
    return lambda e: e.activation(out=out, in_=in_, func=func, **kw)


def AMUL(out, in_, mul):
    return lambda e: e.mul(out, in_, mul)


def TT(out, in0, in1, op):
    return lambda e: e.tensor_tensor(out=out, in0=in0, in1=in1, op=op)


def TS(out, in0, s1, s2, op0, op1=None):
    if op1 is None:
        return lambda e: e.tensor_scalar(out=out, in0=in0, scalar1=s1, scalar2=None, op0=op0)
    return lambda e: e.tensor_scalar(out=out, in0=in0, scalar1=s1, scalar2=s2, op0=op0, op1=op1)


def STT(out, in0, scalar, in1, op0, op1):
    return lambda e: e.scalar_tensor_tensor(out=out, in0=in0, scalar=scalar, in1=in1, op0=op0, op1=op1)


def CP(out, in_):
    return lambda e: e.tensor_copy(out=out, in_=in_)


def RCP(out, in_):
    return lambda e: e.reciprocal(out=out, in_=in_)


def MS(ap, val):
    return lambda e: e.memset(ap, val)


def DMA(out, in_):
    return lambda e: e.dma_start(out=out, in_=in_)


SRC = dict(gq=0, gk=512, gv=1024, gz=1536, gb=2048, ga=2052, fq=2056, fk=2568, fv=3080, ff=3592)
DST = dict(gq=0, gk=512, gv=1024, fq=1536, fk=2048, gz=2560, fv=3072, gb=3584, ga=3588, ff=3592)
WID = dict(gq=512, gk=512, gv=512, gz=512, gb=4, ga=4, fq=512, fk=512, fv=512, ff=8)


def build_nc(NR, phases="AB", dbg=False, stop=None, dumps=()):
    NT = NR + 1
    T = 16 + 128 * NR
    nc = bass.Bass("TRN2", target_bir_lowering=False)

    def dram(name, shape, dt=F32, kind="ExternalInput"):
        return nc.dram_tensor(name, list(shape), dt, kind=kind).ap()

    x_d = dram("x", [NR * 128, D])
    meta_d = dram("meta", [16, D])
    win_d = dram("w_in", [D, DIN])
    wout_d = dram("w_out", [D, D])
    wg_d = dram("w_gate", [D, DFF])
    wu_d = dram("w_up", [D, DFF])
    wd_d = dram("w_down", [DFF, D])
    cf_d = dram("cf32", [128, 7 * 128])
    cb_d = dram("cbf16", [128, 2 * 128], BF16)
    convw_d = dram("convw", [128, 48])
    anw_d = dram("anw", [128, 8])
    fnw_d = dram("fnw", [128, 8])
    finw_d = dram("finw", [128, D])
    gnw_d = dram("gnw", [128, 512])
    raw12_d = dram("raw12", [128, 12])
    sgn12_d = dram("sgn12", [128, 12])
    alog_d = dram("alog", [128, 4])
    out_d = dram("out", [NR * 128, D], kind="ExternalOutput")
    h1_d = dram("h1s", [NR * 128, D], kind="ExternalOutput" if dbg else "Internal")

    es = ExitStack()
    with es:
        NW = 53000
        arena_h = es.enter_context(nc.sbuf_tensor("arena", [128, NW], F32))
        AR = Arena(arena_h, NW)
        pb = [es.enter_context(nc.psum_tensor("pb%d" % i, [128, 512], F32))[:] for i in range(8)]
        pb0b = pb[0].bitcast(BF16)
        pb1b = pb[1].bitcast(BF16)
        S = Sched(nc, es)

        cf = AR.alloc([7, 128], F32)
        identf, TRI, ONES, HALF, POSA, NEGD = (cf[:, k, :] for k in range(6))
        cb = AR.alloc([2, 128], BF16)
        identb, MASK01 = cb[:, 0, :], cb[:, 1, :]
        S.op("sp", DMA(cf, cf_d.rearrange("p (a b) -> p a b", a=7)), writes=["const"], dma="T_c")
        S.op("sp", DMA(cb, cb_d.rearrange("p (a b) -> p a b", a=2)), writes=["const"], dma="T_c")
        common_mark = AR.off

        class StopBuild(Exception):
            pass

        def checkpoint(i, name, env):
            if stop is None or stop != (i, name):
                return
            for dn, fn_, reads in dumps:
                ap = fn_(env)
                d = nc.dram_tensor("dump_" + dn, list(ap.shape), ap.dtype, kind="ExternalOutput").ap()
                S.op("sp", DMA(d, ap), reads=reads, writes=["dump_" + dn], dma="T_dump")
            raise StopBuild()

        def small_load(dst, src):
            S.op("sp", DMA(dst, src), writes=["const"], dma="T_c")

        ev = [0]

        def evac_copy(out, in_, reads, writes, scale=None, eng=None):
            ev[0] += 1
            e = eng if eng is not None else ("act" if ev[0] % 2 else "dve")
            if e == "act":
                fn = ACTF(out, in_, AF.Copy) if scale is None else AMUL(out, in_, scale)
            else:
                fn = CP(out, in_) if scale is None else TS(out, in_, scale, None, ALU.mult)
            S.op(e, fn, reads=reads, writes=writes)

        UT = ["uT%d" % k for k in range(8)]
        UC = ["uc%d" % k for k in range(8)]

        if "A" in phases:
            Win = AR.alloc([8, DIN], BF16)
            Wout = AR.alloc([8, D], BF16)
            KT = AR.alloc([4, T], BF16)
            V = AR.alloc([NT, 8, 65], BF16)
            CC = AR.alloc([NT, 8], F32)
            biasTB = [AR.alloc([NT, 8], F32) for _ in range(2)]
            convw = AR.alloc([12, 4], F32)
            anw = AR.alloc([8], F32)
            gnw = AR.alloc([4, 128], F32)
            raw12 = AR.alloc([12], F32)
            sgn12 = AR.alloc([12], F32)
            bias12 = AR.alloc([12], F32)
            mul12 = AR.alloc([12], F32)
            alog = AR.alloc([4], F32)
            accf = AR.alloc([8], F32)
            xtB = [AR.alloc([D], F32) for _ in range(2)]
            uB = [AR.alloc([D], BF16) for _ in range(2)]
            uTB = [AR.alloc([8, 128], BF16) for _ in range(2)]
            pre = AR.alloc([12, 131], F32)
            cv = AR.alloc([12, 128], F32)
            QTg = AR.alloc([4, 128], BF16)
            KTg = AR.alloc([4, 128], BF16)
            VTg = AR.alloc([4, 128], BF16)
            QTfB = [AR.alloc([4, 128], BF16) for _ in range(2)]
            zsB = [AR.alloc([4, 128], F32) for _ in range(2)]
            sm = AR.alloc([16], F32)
            gtB = [AR.alloc([4, 12], F32) for _ in range(2)]
            sm4B = [AR.alloc([9, 4], F32) for _ in range(2)]
            ssq = AR.alloc([4], F32)
            so = AR.alloc([8], F32)
            crefB = [AR.alloc([8], F32) for _ in range(2)]
            rd = AR.alloc([4], F32)
            pgsB = [AR.alloc([24], F32) for _ in range(2)]
            junkb = AR.alloc([128], BF16)
            Bm4 = AR.alloc([4, 128], F32)
            Bm = [Bm4[:, q_, :] for q_ in range(4)]
            BTm4 = AR.alloc([4, 128], F32)
            BTm = [BTm4[:, q_, :] for q_ in range(4)]
            PTm4 = AR.alloc([4, 128], F32)
            PTm = [PTm4[:, q_, :] for q_ in range(4)]
            TTb4 = AR.alloc([4, 128], BF16)
            TTb = [TTb4[:, q_, :] for q_ in range(4)]
            kw = [AR.alloc([128], BF16) for _ in range(4)]
            kd = [AR.alloc([128], BF16) for _ in range(4)]
            vb = [AR.alloc([128], BF16) for _ in range(4)]
            wT4 = AR.alloc([4, 128], BF16)
            wT = [wT4[:, q_, :] for q_ in range(4)]
            vn4 = AR.alloc([4, 128], BF16)
            vn = [vn4[:, q_, :] for q_ in range(4)]
            QgT4 = AR.alloc([4, 128], BF16)
            QgT = [QgT4[:, q_, :] for q_ in range(4)]
            qkmT4 = AR.alloc([4, 128], BF16)
            qkmT = [qkmT4[:, q_, :] for q_ in range(4)]
            Sbf4 = AR.alloc([4, 128], BF16)
            Sbf = [Sbf4[:, q_, :] for q_ in range(4)]
            usb4 = AR.alloc([4, 128], F32)
            usb = [usb4[:, q_, :] for q_ in range(4)]
            Sst4 = AR.alloc([4, 128], F32)
            Sst = [Sst4[:, q_, :] for q_ in range(4)]
            pT = [[AR.alloc([128], BF16) for _ in range(4)] for _ in range(2)]
            Gb, DA4, DA, DT4, DT, eg4, eg = Bm, PTm4, PTm, usb4, usb, BTm4, BTm
            print("phase A arena words used", AR.off, "of", NW)

            for nm in ("gq", "gk", "gv", "fq", "fk", "gz", "fv", "gb", "ga", "ff"):
                s0, d0, wd = SRC[nm], DST[nm], WID[nm]
                S.op("pool", DMA(Win[:, :, d0:d0 + wd],
                                 win_d.rearrange("(k p) c -> p k c", p=128)[:, :, s0:s0 + wd]),
                     writes=["Win_" + nm], dma="T_w_" + nm)
            S.op("pool", DMA(Wout, wout_d.rearrange("(k p) c -> p k c", p=128)), writes=["Wout"], dma="T_wo")
            small_load(convw, convw_d.rearrange("p (a b) -> p a b", a=12))
            small_load(anw, anw_d)
            small_load(gnw, gnw_d.rearrange("p (a b) -> p a b", a=4))
            small_load(raw12, raw12_d)
            small_load(sgn12, sgn12_d)
            small_load(alog, alog_d)
            S.op("pool", MS(V[:, :, :, 64:65], 1.0), writes=["V%d" % t_ for t_ in range(NT)])
            S.op("pool", MS(pre, 0.0), writes=["pre"])
            S.op("pool", MS(accf, 0.0), writes=["accf"])
            S.op("pool", MS(CC, 0.0), writes=["CC"])
            for h in range(4):
                S.op("pool", MS(Sst[h], 0.0), writes=["S%d" % h])
                S.op("pool", MS(Sbf[h], 0.0), writes=["Sbf%d" % h])
            S.op("dve", TT(bias12, raw12, sgn12, ALU.mult), reads=["const"], writes=["g12"])
            S.op("act", ACTF(mul12[:, 0:4], alog, AF.Exp), reads=["const"], writes=["m12a"])
            S.op("dve", TS(mul12[:, 0:4], mul12[:, 0:4], -1.0, None, ALU.mult), reads=["m12a"], writes=["m12"])
            S.op("dve", MS(mul12[:, 4:12], -1.0), writes=["m12b"])

            def pf_gen(i):
                n = 16 if i == 0 else 128
                pos = 0 if i == 0 else 16 + 128 * (i - 1)
                p_ = i % 2
                xt, u, uT, QTf, zs = xtB[p_], uB[p_], uTB[p_], QTfB[p_], zsB[p_]
                XT, QTFN, ZSN = "xt%d" % p_, "QTf%d" % p_, "zs%d" % p_
                UC = ["uc%d_%d" % (p_, k) for k in range(8)]
                UT = ["uT%d_%d" % (p_, k) for k in range(8)]
                sm4, gt, pgs, cref, biasT = sm4B[p_], gtB[p_], pgsB[p_], crefB[p_], biasTB[p_]
                G_ = "_%d" % p_
                gcol = gt[:, 3, 0:4]
                lfcol = gt[:, 3, 4:12]
                beta, nbeta, ngc, egc, bege, egl, ekd, gtmp = (sm4[:, k, :] for k in range(1, 9))
                src = meta_d if i == 0 else x_d[(i - 1) * 128:i * 128, :]
                S.op("sp", DMA(xt[:n], src), writes=[XT], dma="xld%d" % p_)
                yield
                S.op("pool", MS(ssq[:, 0:1], 0.0), writes=["ssq"])
                S.op("act", ACTF(u[:n], xt[:n], AF.Square, accum_out=ssq[:n, 0:1]), reads=[XT], writes=UC + ["ssq"])
                S.op("dve", TS(ssq[:n, 1:2], ssq[:n, 0:1], 1.0 / D, EPS, ALU.mult, ALU.add), reads=["ssq"], writes=["ssq1"])
                S.op("act", ACTF(ssq[:n, 2:3], ssq[:n, 1:2], AF.Ln), reads=["ssq1"], writes=["ssq2"])
                S.op("act", ACTF(ssq[:n, 3:4], ssq[:n, 2:3], AF.Exp, scale=-0.5), reads=["ssq2"], writes=["rstd"])
                S.op("dve", TS(u[:n], xt[:n], ssq[:n, 3:4], None, ALU.mult), reads=[XT, "rstd"], writes=UC)
                yield
                for kc in range(8):
                    S.op("pe", TR(pb1b[:, kc * 128:kc * 128 + n], u[:n, kc * 128:(kc + 1) * 128], identb[:n, :n]),
                         reads=[UC[kc], "const"], writes=["P1"])
                e3 = "act" if i % 2 else "dve"
                for kc in range(8):
                    evac_copy(uT[:, kc, :n], pb1b[:, kc * 128:kc * 128 + n], ["P1", "const"], [UT[kc]],
                              scale=anw[:, kc:kc + 1], eng=e3)
                yield
                for g in range(5):
                    bi = 2 - g % 2
                    bank, bn = pb[bi], "P%d" % bi
                    for c4 in range(4):
                        c = 4 * g + c4
                        for kc in range(8):
                            S.op("pe", MM(bank[:, c4 * 128:c4 * 128 + n], Win[:, kc, c * 128:(c + 1) * 128],
                                          uT[:, kc, :n], start=(kc == 0), stop=(kc == 7)),
                                 reads=[UT[kc], "Win_" + ("gq", "gk", "gv", "fq", "fk")[g]], writes=[bn])
                    srcp = bank.rearrange("p (c t) -> p c t", c=4)[:, :, :n]
                    yield
                    if g < 3:
                        evac_copy(pre[:, 4 * g:4 * g + 4, 3:3 + n], srcp, [bn], ["pre"])
                    elif g == 3:
                        evac_copy(QTf[:, :, :n], srcp, [bn], [QTFN], scale=0.125)
                    else:
                        evac_copy(KT[:, :, pos:pos + n], srcp, [bn], ["KT%d" % i])
                yield
                if i > 0:
                    for kc in range(8):
                        S.op("pe", MM(pb[1][:n, :], uT[:, kc, :n], Win[:, kc, 2560:3072], start=(kc == 0), stop=(kc == 7)),
                             reads=[UT[kc], "Win_gz"], writes=["P1"])
                    S.op("act", ACTF(zs[:n].rearrange("p a b -> p (a b)"), pb[1][:n, :], AF.Silu), reads=["P1"], writes=[ZSN])
                    S.op("pool", TT(zs[:n], zs[:n], gnw[:n], ALU.mult), reads=[ZSN, "const"], writes=[ZSN])
                for kc in range(8):
                    S.op("pe", MM(pb[2][:n, :], uT[:, kc, :n], Win[:, kc, 3072:3584], start=(kc == 0), stop=(kc == 7)),
                         reads=[UT[kc], "Win_fv"], writes=["P2"])
                evac_copy(V[:n, i, :, 0:64], pb[2][:n, :].rearrange("p (h d) -> p h d", h=8), ["P2"], ["V%d" % i])
                for kc in range(8):
                    S.op("pe", MM(pb[1][:n, 0:16], uT[:, kc, :n], Win[:, kc, 3584:3600], start=(kc == 0), stop=(kc == 7)),
                         reads=[UT[kc], "Win_gb", "Win_ga", "Win_ff"], writes=["P1"])
                S.op("dve", CP(sm[:n], pb[1][:n, 0:16]), reads=["P1"], writes=["sm"])
                yield
                S.op("dve", TT(gt[:n, 0, :], sm[:n, 4:16], sgn12[:n], ALU.mult), reads=["sm", "const"], writes=["gt0" + G_])
                S.op("dve", TT(gt[:n, 0, :], gt[:n, 0, :], bias12[:n], ALU.add), reads=["gt0" + G_, "g12"], writes=["gt0" + G_])
                S.op("act", ACTF(gt[:n, 1, :], gt[:n, 0, :], AF.Exp), reads=["gt0" + G_], writes=["gt1" + G_])
                S.op("dve", TS(gt[:n, 1, :], gt[:n, 1, :], 1.0, None, ALU.add), reads=["gt1" + G_], writes=["gt1" + G_])
                S.op("act", ACTF(gt[:n, 2, :], gt[:n, 1, :], AF.Ln), reads=["gt1" + G_], writes=["gt2" + G_])
                S.op("dve", TT(gt[:n, 3, :], gt[:n, 2, :], mul12[:n], ALU.mult), reads=["gt2" + G_, "m12", "m12b"], writes=["GL" + G_])
                S.op("act", ACTF(sm4[:n, 0, :], sm[:n, 0:4], AF.Exp, scale=-1.0), reads=["sm"], writes=["b0" + G_])
                S.op("dve", TS(sm4[:n, 0, :], sm4[:n, 0, :], 1.0, None, ALU.add), reads=["b0" + G_], writes=["b0" + G_])
                S.op("dve", RCP(sm4[:n, 1, :], sm4[:n, 0, :]), reads=["b0" + G_], writes=["beta" + G_])
                S.op("dve", TS(sm4[:n, 2, :], sm4[:n, 1, :], -1.0, None, ALU.mult), reads=["beta" + G_], writes=["nbeta" + G_])
                pg = pb[2]
                S.op("pe", MM(pg[:n, 0:4], TRI[:n, :n], gcol[:n]), reads=["GL" + G_, "const"], writes=["P2"])
                S.op("pe", MM(pg[:, 4:8], ONES[:n, :], gcol[:n]), reads=["GL" + G_, "const"], writes=["P2"])
                S.op("pe", MM(pg[:n, 8:16], TRI[:n, :n], lfcol[:n], start=True, stop=False), reads=["GL" + G_, "const"], writes=["P2"])
                S.op("pe", MM(pg[:n, 8:16], ONES[:, :n], accf, start=False, stop=True), reads=["accf", "const"], writes=["P2"])
                hsel = ONES if i == 0 else HALF
                S.op("pe", MM(pg[:, 16:24], hsel[:n, :], lfcol[:n], start=True, stop=False), reads=["GL" + G_, "const"], writes=["P2"])
                S.op("pe", MM(pg[:, 16:24], ONES, accf, start=False, stop=True), reads=["accf", "const"], writes=["P2"])
                S.op("dve", CP(pgs[:, 4:8], pg[:, 4:8]), reads=["P2"], writes=["pgs" + G_])
                S.op("dve", CP(pgs[:, 16:24], pg[:, 16:24]), reads=["P2"], writes=["pgs" + G_])
                S.op("dve", CP(pgs[:n, 0:4], pg[:n, 0:4]), reads=["P2"], writes=["pgs" + G_])
                S.op("dve", CP(pgs[:n, 8:16], pg[:n, 8:16]), reads=["P2"], writes=["pgs" + G_])
                S.op("dve", TS(ngc[:n], pgs[:n, 0:4], -1.0, None, ALU.mult), reads=["pgs" + G_], writes=["ngc" + G_])
                S.op("act", ACTF(egc[:n], pgs[:n, 0:4], AF.Exp), reads=["pgs" + G_], writes=["egc" + G_])
                S.op("dve", TT(bege[:n], beta[:n], egc[:n], ALU.mult), reads=["beta" + G_, "egc" + G_], writes=["bege" + G_])
                S.op("act", ACTF(egl, pgs[:, 4:8], AF.Exp), reads=["pgs" + G_], writes=["egl" + G_])
                S.op("dve", TT(gtmp[:n], pgs[:n, 4:8], ngc[:n], ALU.add), reads=["pgs" + G_, "ngc" + G_], writes=["gtmp" + G_])
                S.op("act", ACTF(ekd[:n], gtmp[:n], AF.Exp), reads=["gtmp" + G_], writes=["ekd" + G_])
                S.op("pool", CP(CC[:n, i, :], pgs[:n, 8:16]), reads=["pgs" + G_], writes=["CC"])
                S.op("pool", CP(cref, pgs[:, 16:24]), reads=["pgs" + G_], writes=["cref" + G_])
                S.op("pool", TT(accf[:n], accf[:n], lfcol[:n], ALU.add), reads=["GL" + G_, "accf"], writes=["accf"])
                if i > 0:
                    for h in range(8):
                        S.op("pool", TS(biasT[:, 0:i + 1, h], CC[:, 0:i + 1, h], -1.0, cref[:, h:h + 1], ALU.mult, ALU.add),
                             reads=["CC", "cref" + G_], writes=["biasT" + G_])

                yield
                for c in range(12):
                    yield
                    e = "dve"
                    S.op(e, TS(cv[:, c, :n], pre[:, c, 3:3 + n], convw[:, c, 3:4], None, ALU.mult),
                         reads=["pre", "const"], writes=["cv%d" % c])
                    for k in range(3):
                        if e == "dve":
                            S.op(e, STT(cv[:, c, :n], pre[:, c, k:k + n], convw[:, c, k:k + 1], cv[:, c, :n], ALU.mult, ALU.add),
                                 reads=["pre", "const", "cv%d" % c], writes=["cv%d" % c])
                        else:
                            S.op(e, TS(ctmp[:, :n], pre[:, c, k:k + n], convw[:, c, k:k + 1], None, ALU.mult),
                                 reads=["pre", "const"], writes=["ctmp"])
                            S.op(e, TT(cv[:, c, :n], cv[:, c, :n], ctmp[:, :n], ALU.add),
                                 reads=["ctmp", "cv%d" % c], writes=["cv%d" % c])
                cvall = ["cv%d" % c for c in range(12)]
                S.op("dve", CP(pre[:, :, 0:3], pre[:, :, n:n + 3]), reads=["pre"], writes=["pre"])
                S.op("act", ACTF(cv[:, 0:8, :n], cv[:, 0:8, :n], AF.Silu), reads=cvall, writes=cvall + ["cvs"])
                S.op("act", ACTF(VTg[:, :, :n], cv[:, 8:12, :n], AF.Silu), reads=cvall, writes=["VTg"])
                sq = pre[:, 0:8, 3:3 + n]
                S.op("dve", TT(sq, cv[:, 0:8, :n], cv[:, 0:8, :n], ALU.mult), reads=["cvs"], writes=["pre"])
                for _ in range(6):
                    yield
                for g2 in range(2):
                    bank, bn = pb[1 + g2], "P%d" % (1 + g2)
                    for c4 in range(4):
                        S.op("pe", MM(bank[:, c4 * 128:c4 * 128 + n], ONES, pre[:, 4 * g2 + c4, 3:3 + n]),
                             reads=["pre", "const"], writes=[bn])
                yield
                for g2 in range(2):
                    bank, bn = pb[1 + g2], "P%d" % (1 + g2)
                    srcp = bank.rearrange("p (c t) -> p c t", c=4)[:, :, :n]
                    dst = pre[:, 4 * g2:4 * g2 + 4, 3:3 + n]
                    S.op("dve", TS(dst, srcp, EPS, None, ALU.add), reads=[bn], writes=["pre"])
                    S.op("act", ACTF(dst, dst, AF.Ln), reads=["pre"], writes=["pre"])
                    S.op("act", ACTF(dst, dst, AF.Exp, scale=-0.5), reads=["pre"], writes=["pre"])
                S.op("dve", STT(QTg[:, :, :n], cv[:, 0:4, :n], 128.0 ** -0.5, pre[:, 0:4, 3:3 + n], ALU.mult, ALU.mult),
                     reads=["cvs", "pre"], writes=["QTg"])
                S.op("dve", TT(KTg[:, :, :n], cv[:, 4:8, :n], pre[:, 4:8, 3:3 + n], ALU.mult), reads=["cvs", "pre"], writes=["KTg"])

            WIN_NAMES = ["Win_" + nm for nm in ("gq", "gk", "gv", "fq", "fk", "gz", "fv", "gb", "ga", "ff")]
            Wg_early = Win.rearrange("p k c -> p (k c)")[:, 0:8 * DFF].rearrange("p (k c) -> p k c", k=8)

            def tile_body(i):
                n = 16 if i == 0 else 128
                pos = 0 if i == 0 else 16 + 128 * (i - 1)
                if i == NT - 1 and "B" in phases and stop is None:
                    S.op("pool", DMA(Wg_early, wg_d.rearrange("(k p) c -> p k c", p=128)), writes=WIN_NAMES + ["Wg"], dma="T_w2")
                p_ = i % 2
                xt, u, uT, QTf, zs = xtB[p_], uB[p_], uTB[p_], QTfB[p_], zsB[p_]
                XT, QTFN, ZSN = "xt%d" % p_, "QTf%d" % p_, "zs%d" % p_
                UC = ["uc%d_%d" % (p_, k) for k in range(8)]
                UT = ["uT%d_%d" % (p_, k) for k in range(8)]
                sm4, gt, pgs, cref, biasT = sm4B[p_], gtB[p_], pgsB[p_], crefB[p_], biasTB[p_]
                G_ = "_%d" % p_
                gcol = gt[:, 3, 0:4]
                lfcol = gt[:, 3, 4:12]
                beta, nbeta, ngc, egc, bege, egl, ekd, gtmp = (sm4[:, k, :] for k in range(1, 9))
                def gdn_gen():
                    yield
                    for h in range(4):
                        S.op("pe", MM(pb[1][:n, h * 128:(h + 1) * 128], KTg[:, h, :n], identb), reads=["KTg", "const"], writes=["P1"])
                    yield
                    for h in range(4):
                        S.op("pe", MM(pb[2][:n, h * 128:(h + 1) * 128], VTg[:, h, :n], identb), reads=["VTg", "const"], writes=["P2"])
                    yield
                    for h in range(4):
                        S.op("dve", TS(kw[h][:n], pb[1][:n, h * 128:(h + 1) * 128], bege[:n, h:h + 1], None, ALU.mult),
                             reads=["P1", "bege" + G_], writes=["kw%d" % h])
                        S.op("dve", TS(kd[h][:n], pb[1][:n, h * 128:(h + 1) * 128], ekd[:n, h:h + 1], None, ALU.mult),
                             reads=["P1", "ekd" + G_], writes=["kd%d" % h])
                        S.op("act", AMUL(vb[h][:n], pb[2][:n, h * 128:(h + 1) * 128], beta[:n, h:h + 1]),
                             reads=["P2", "beta" + G_], writes=["vb%d" % h])
                    checkpoint(i, "prep", locals())
                    yield "PREP_DONE"
                    yield
                    for h in range(4):
                        S.op("dve", TS(Gb[h][:n], ONES[:n], gcol[:n, h:h + 1], None, ALU.mult), reads=["GL" + G_, "const"], writes=["B%d" % h])
                        S.op("pe", MM(pb[4][:, h * 128:h * 128 + n], Gb[h][:n], TRI[:n, :n]), reads=["B%d" % h, "const"], writes=["P4"])
                    yield
                    for h in range(4):
                        S.op("pe", MM(pb[5][:n, h * 128:h * 128 + n], KTg[:, h, :n], KTg[:, h, :n]), reads=["KTg"], writes=["P5"])
                    if i > 0:
                        for h in range(4):
                            S.op("pe", MM(pb[6][:n, h * 128:h * 128 + n], KTg[:, h, :n], QTg[:, h, :n]), reads=["KTg", "QTg"], writes=["P6"])
                    yield
                    S.op("dve", CP(eg4[:, :, :n], pb[4].rearrange("p (h c) -> p h c", h=4)[:, :, :n]), reads=["P4"], writes=["BT%d" % q for q in range(4)])
                    yield
                    for h in range(4):
                        gr = eg[h][:n, :n]
                        S.op("dve", STT(DA[h][:n, :n], gr, ngc[:n, h:h + 1], POSA[:n, :n], ALU.add, ALU.add),
                             reads=["BT%d" % h, "ngc" + G_, "const"], writes=["PT%d" % h])
                    S.op("act", ACTF(DA4[:n, :, :n], DA4[:n, :, :n], AF.Exp, scale=-1.0), reads=["PT%d" % q for q in range(4)], writes=["PT%d" % q for q in range(4)])
                    for h in range(4):
                        gr = eg[h][:n, :n]
                        S.op("dve", STT(DT[h][:n, :n], gr, ngc[:n, h:h + 1], NEGD[:n, :n], ALU.add, ALU.add),
                             reads=["BT%d" % h, "ngc" + G_, "const"], writes=["usb%d" % h])
                    S.op("act", ACTF(DT4[:n, :, :n], DT4[:n, :, :n], AF.Exp), reads=["usb%d" % q for q in range(4)], writes=["usb%d" % q for q in range(4)])
                    if i > 0:
                        S.op("act", ACTF(eg4[:, :, :n], eg4[:, :, :n], AF.Exp), reads=["BT%d" % q for q in range(4)], writes=["BT%d" % q for q in range(4)])
                        S.op("pool", TT(QgT4[:, :, :n], QTg[:, :, :n], eg4[:, :, :n], ALU.mult), reads=["QTg"] + ["BT%d" % q for q in range(4)],
                             writes=["QgT%d" % q for q in range(4)])
                    yield
                    for h in range(4):
                        S.op("dve", STT(Bm[h][:n, :n], pb[5][:n, h * 128:h * 128 + n], nbeta[:n, h:h + 1], DA[h][:n, :n], ALU.mult, ALU.mult),
                             reads=["P5", "nbeta" + G_, "PT%d" % h], writes=["B%d" % h])
                    if i > 0:
                        S.op("dve", TT(qkmT4[:n, :, :n], pb[6].rearrange("p (h c) -> p h c", h=4)[:n, :, :n], DT4[:n, :, :n], ALU.mult),
                             reads=["P6"] + ["usb%d" % q for q in range(4)], writes=["qkmT%d" % q for q in range(4)])
                    yield
                    for h in range(4):
                        S.op("pe", TR(pb[4][:n, h * 128:h * 128 + n], Bm[h][:n, :n], identf[:n, :n]), reads=["B%d" % h, "const"], writes=["P4"])
                    yield
                    S.op("dve", CP(BTm4[:n, :, :n], pb[4].rearrange("p (h c) -> p h c", h=4)[:n, :, :n]), reads=["P4"], writes=["BT%d" % q for q in range(4)])
                    for h in range(4):
                        S.op("dve", TT(PTm[h][:n, :n], BTm[h][:n, :n], identf[:n, :n], ALU.add), reads=["BT%d" % h, "const"], writes=["PT%d" % h])
                    checkpoint(i, "gdn7a", locals())
                    nst = 6 if i > 0 else 3
                    yield
                    for k in range(nst):
                        last = (k == nst - 1)
                        yield
                        for h in range(4):
                            S.op("pe", MM(pb[4][:n, h * 128:h * 128 + n], BTm[h][:n, :n], Bm[h][:n, :n]),
                                 reads=["BT%d" % h, "B%d" % h], writes=["P4"])
                        if not last:
                            for h in range(4):
                                S.op("pe", MM(pb[5][:n, h * 128:h * 128 + n], Bm[h][:n, :n], BTm[h][:n, :n]),
                                     reads=["BT%d" % h, "B%d" % h], writes=["P5"])
                        yield
                        for h in range(4):
                            S.op("dve", CP(Bm[h][:n, :n], pb[4][:n, h * 128:h * 128 + n]), reads=["P4"], writes=["B%d" % h])
                        if not last:
                            S.op("act", ACTF(BTm4[:n, :, :n], pb[5].rearrange("p (h c) -> p h c", h=4)[:n, :, :n], AF.Copy), reads=["P5"], writes=["BT%d" % q for q in range(4)])
                        yield
                        for h in range(4):
                            S.op("pe", MM(pb[6][:n, h * 128:h * 128 + n], Bm[h][:n, :n], PTm[h][:n, :n]),
                                 reads=["B%d" % h, "PT%d" % h], writes=["P6"])
                        yield
                        for h in range(4):
                            S.op("dve", TT(PTm[h][:n, :n], pb[6][:n, h * 128:h * 128 + n], PTm[h][:n, :n], ALU.add),
                                 reads=["P6", "PT%d" % h], writes=["PT%d" % h])
                    yield
                    S.op("dve", CP(TTb4[:n, :, :n], PTm4[:n, :, :n]), reads=["PT%d" % q for q in range(4)], writes=["TTb%d" % q for q in range(4)])
                    yield
                    for h in range(4):
                        S.op("pe", MM(pb[4][:n, h * 128:(h + 1) * 128], TTb[h][:n, :n], vb[h][:n, :]), reads=["TTb%d" % h, "vb%d" % h], writes=["P4"])
                    yield
                    for h in range(4):
                        S.op("pe", MM(pb[5][:, h * 128:h * 128 + n], kw[h][:n, :], TTb[h][:n, :n]), reads=["TTb%d" % h, "kw%d" % h], writes=["P5"])
                    yield
                    S.op("dve", CP(usb4[:n], pb[4].rearrange("p (h c) -> p h c", h=4)[:n]), reads=["P4"], writes=["usb%d" % q for q in range(4)])
                    yield
                    S.op("dve", CP(wT4[:, :, :n], pb[5].rearrange("p (h c) -> p h c", h=4)[:, :, :n]), reads=["P5"], writes=["wT%d" % q for q in range(4)])
                    checkpoint(i, "gdn7", locals())
                    yield
                    for h in range(4):
                        S.op("pe", MM(pb[6][:n, h * 128:(h + 1) * 128], wT[h][:, :n], Sbf[h]), reads=["wT%d" % h, "Sbf%d" % h], writes=["P6"])
                    yield
                    S.op("dve", TT(vn4[:n], usb4[:n], pb[6].rearrange("p (h c) -> p h c", h=4)[:n], ALU.subtract),
                         reads=["P6"] + ["usb%d" % q for q in range(4)], writes=["vn%d" % q for q in range(4)])
                    if i > 0:
                        for h in range(4):
                            S.op("pe", MM(pb[4][:n, h * 128:(h + 1) * 128], QgT[h][:, :n], Sbf[h], start=True, stop=False),
                                 reads=["QgT%d" % h, "Sbf%d" % h], writes=["P4"])
                            S.op("pe", MM(pb[4][:n, h * 128:(h + 1) * 128], qkmT[h][:n, :n], vn[h][:n, :], start=False, stop=True),
                                 reads=["qkmT%d" % h, "vn%d" % h], writes=["P4"])
                    yield
                    for h in range(4):
                        S.op("pe", MM(pb[5][:, h * 128:(h + 1) * 128], kd[h][:n, :], vn[h][:n, :]), reads=["kd%d" % h, "vn%d" % h], writes=["P5"])
                    if i > 0:
                        S.op("pool", MS(so[:, 0:4], 0.0), writes=["so"])
                        for h in range(4):
                            S.op("act", ACTF(junkb[:n, :], pb[4][:n, h * 128:(h + 1) * 128], AF.Square, accum_out=so[:n, h:h + 1]),
                                 reads=["P4"], writes=["so", "junkb"])
                        S.op("dve", TS(so[:n, 4:8], so[:n, 0:4], 1.0 / 128, EPS, ALU.mult, ALU.add), reads=["so"], writes=["so"])
                        S.op("act", ACTF(so[:n, 4:8], so[:n, 4:8], AF.Ln), reads=["so"], writes=["so"])
                        S.op("act", ACTF(so[:n, 4:8], so[:n, 4:8], AF.Exp, scale=-0.5), reads=["so"], writes=["so"])
                        for h in range(4):
                            S.op("act", AMUL(usb[h][:n, :], pb[4][:n, h * 128:(h + 1) * 128], so[:n, 4 + h:5 + h]),
                                 reads=["P4", "so"], writes=["usb%d" % h])
                    if i > 0:
                        S.op("dve", TT(u[:n, 0:512].rearrange("p (h c) -> p h c", h=4), usb4[:n], zs[:n], ALU.mult),
                             reads=["usb%d" % q for q in range(4)] + [ZSN], writes=UC[0:4])
                    yield
                    for h in range(4):
                        S.op("dve", STT(Sst[h], Sst[h], egl[:, h:h + 1], pb[5][:, h * 128:(h + 1) * 128], ALU.mult, ALU.add),
                             reads=["P5", "egl" + G_, "S%d" % h], writes=["S%d" % h])
                    S.op("pool", CP(Sbf4, Sst4), reads=["S%d" % q for q in range(4)], writes=["Sbf%d" % q for q in range(4)])
                    yield
                def attn_gen():
                    items = []
                    for h in range(8):
                        kts = list(range(i + 1))
                        for c0 in range(0, len(kts), 4):
                            items.append((h, kts[c0:c0 + 4]))

                    def scores(idx):
                        h, ks = items[idx]
                        hp, r0 = h // 2, 64 * (h % 2)
                        bi = (0, 7)[idx % 2]
                        st_ = idx % 2
                        for j, kt in enumerate(ks):
                            nk = 16 if kt == 0 else 128
                            kpos = 0 if kt == 0 else 16 + 128 * (kt - 1)
                            S.op("pe", MM(pb[bi][:nk, j * 128:(j + 1) * 128], KT[r0:r0 + 64, hp, kpos:kpos + nk], QTf[r0:r0 + 64, hp, :]),
                                 reads=["KT%d" % kt, QTFN], writes=["P%d" % bi])
                        for j, kt in enumerate(ks):
                            nk = 16 if kt == 0 else 128
                            S.op("act", ACTF(pT[st_][j][:nk, :], pb[bi][:nk, j * 128:(j + 1) * 128], AF.Exp, bias=biasT[:nk, kt, h:h + 1], scale=1.0),
                                 reads=["P%d" % bi, "biasT" + G_], writes=["pT%d_%d" % (st_, j)])
                            if kt == i:
                                S.op("dve", TT(pT[st_][j][:nk, :], pT[st_][j][:nk, :], MASK01[:nk, :], ALU.mult),
                                     reads=["pT%d_%d" % (st_, j), "const"], writes=["pT%d_%d" % (st_, j)])

                    def pv(idx):
                        h, ks = items[idx]
                        st_ = idx % 2
                        hh = h % 4
                        fb = 3
                        for j, kt in enumerate(ks):
                            nk = 16 if kt == 0 else 128
                            S.op("pe", MM(pb[fb][:, hh * 65:(hh + 1) * 65], pT[st_][j][:nk, :], V[:nk, kt, h, :], start=(kt == 0), stop=(kt == i)),
                                 reads=["pT%d_%d" % (st_, j), "V%d" % kt], writes=["P%d" % fb])
                        if ks[-1] == i and hh == 3:
                            fo3 = pb[fb][:, 0:260].rearrange("p (h d) -> p h d", h=4)
                            S.op("dve", RCP(rd, fo3[:, :, 64]), reads=["P%d" % fb], writes=["rd"])
                            for q in range(4):
                                hq = h - 3 + q
                                evac_copy(u[:, 512 + hq * 64:512 + (hq + 1) * 64], fo3[:, q, 0:64], ["P%d" % fb, "rd"], [UC[4 + hq // 2]],
                                          scale=rd[:, q:q + 1], eng="dve")

                    scores(0)
                    for idx in range(len(items)):
                        if idx + 1 < len(items):
                            scores(idx + 1)
                        pv(idx)
                        yield
                def run_streams(gens, late=None, first=None):
                    gens = list(gens)
                    prep_done = False
                    while gens:
                        for g_ in list(gens):
                            try:
                                v_ = next(g_)
                            except StopIteration:
                                gens.remove(g_)
                                if g_ is first:
                                    first = None
                                continue
                            if v_ == "PREP_DONE":
                                prep_done = True
                        if late is not None and prep_done and first is None:
                            gens.append(late)
                            late = None
                    if late is not None:
                        for _ in late:
                            pass
                nxt = pf_gen(i + 1) if i + 1 < NT else None
                if i == 0:
                    run_streams([gdn_gen()], late=nxt)
                    return
                def out_gen():
                    for kc in range(8):
                        S.op("pe", TR(pb0b[:, kc * 128:(kc + 1) * 128], u[:, kc * 128:(kc + 1) * 128], identb), reads=[UC[kc], "const"], writes=["P0"])
                    for kc in range(8):
                        evac_copy(uT[:, kc, :], pb0b[:, kc * 128:(kc + 1) * 128], ["P0"], [UT[kc]], eng=("dve" if i % 2 else "act"))
                    yield
                    for half in range(2):
                        bi_ = (7, 0)[half]
                        bank, bn = pb[bi_], "P%d" % bi_
                        for kc in range(8):
                            S.op("pe", MM(bank, uT[:, kc, :], Wout[:, kc, half * 512:(half + 1) * 512], start=(kc == 0), stop=(kc == 7)),
                                 reads=[UT[kc], "Wout"], writes=[bn])
                        S.op("dve", TT(xt[:, half * 512:(half + 1) * 512], xt[:, half * 512:(half + 1) * 512], bank, ALU.add),
                             reads=[bn, XT], writes=[XT])
                        if half == 0:
                            yield
                    S.op("sp", DMA(h1_d[(i - 1) * 128:i * 128, :], xt), reads=[XT], writes=["h1d"], dma="h1st%d" % p_)

                prev_out = pending_out[0]
                pending_out[0] = out_gen()
                run_streams(([prev_out] if prev_out is not None else []) + [gdn_gen(), attn_gen()], late=nxt, first=prev_out)

            pending_out = [None]
            try:
                for _ in pf_gen(0):
                    pass
                for i in range(NT):
                    tile_body(i)
                if pending_out[0] is not None:
                    for _ in pending_out[0]:
                        pass
            except StopBuild:
                pass

        if "B" in phases and stop is None:
            S.barrier()
            AR.off = common_mark
            Wg = AR.alloc([8, DFF], BF16)
            Wu = AR.alloc([8, DFF], BF16)
            Wd = AR.alloc([NFC, D], BF16)
            fnw = AR.alloc([8], F32)
            finw = AR.alloc([D], F32)
            hb = AR.alloc([2, D], F32)
            hb2 = AR.alloc([2, D], F32)
            u2T2 = AR.alloc([8, 256], BF16)
            junk2 = AR.alloc([D], BF16)
            u2 = AR.alloc([D], BF16)
            u2T = AR.alloc([8, 256], BF16)
            actT = AR.alloc([NFC, 256], BF16)
            sg = [AR.alloc([256], F32) for _ in range(2)]
            st = AR.alloc([8], F32)
            print("phase B arena words used", AR.off, "of", NW)
            if "A" not in phases:
                S.op("pool", DMA(Wg, wg_d.rearrange("(k p) c -> p k c", p=128)), writes=["Wg"], dma="T_w2")
            S.op("pool", DMA(Wu, wu_d.rearrange("(k p) c -> p k c", p=128)), writes=["Wu"], dma="T_w2")
            S.op("pool", DMA(Wd, wd_d.rearrange("(k p) c -> p k c", p=128)), writes=["Wd"], dma="T_w2")
            S.op("sp", DMA(fnw, fnw_d), writes=["constB"], dma="T_c2")
            S.op("sp", DMA(finw, finw_d), writes=["constB"], dma="T_c2")
            NG = NR // 2
            hbB = [hb, hb2]
            u2TB = [u2T, u2T2]

            def prepB(g):
                q_ = g % 2
                hbq, u2Tq = hbB[q_], u2TB[q_]
                S.op("sp", DMA(hbq, h1_d[g * 256:(g + 1) * 256, :].rearrange("(t p) d -> p t d", p=128)),
                     reads=["h1d"], writes=["hb%d" % q_], dma="hld%d" % q_)
                for j in range(2):
                    S.op("pool", MS(st[:, 0:1], 0.0), writes=["st"])
                    S.op("act", ACTF(u2, hbq[:, j, :], AF.Square, accum_out=st[:, 0:1]), reads=["hb%d" % q_], writes=["u2", "st"])
                    S.op("dve", TS(st[:, 1:2], st[:, 0:1], 1.0 / D, EPS, ALU.mult, ALU.add), reads=["st"], writes=["st"])
                    S.op("act", ACTF(st[:, 2:3], st[:, 1:2], AF.Ln), reads=["st"], writes=["st"])
                    S.op("act", ACTF(st[:, 3:4], st[:, 2:3], AF.Exp, scale=-0.5), reads=["st"], writes=["st"])
                    S.op("dve", TS(u2, hbq[:, j, :], st[:, 3:4], None, ALU.mult), reads=["hb%d" % q_, "st"], writes=["u2"])
                    for kc in range(8):
                        S.op("pe", TR(pb0b[:, kc * 128:(kc + 1) * 128], u2[:, kc * 128:(kc + 1) * 128], identb),
                             reads=["u2", "const"], writes=["P0"])
                    for kc in range(8):
                        evac_copy(u2Tq[:, kc, j * 128:(j + 1) * 128], pb0b[:, kc * 128:(kc + 1) * 128], ["P0", "constB"],
                                  ["u2T%d_%d" % (q_, kc)], scale=fnw[:, kc:kc + 1], eng=("act" if j else "dve"))

            prepB(0)
            for g in range(NG):
                q_ = g % 2
                hbq, u2Tq = hbB[q_], u2TB[q_]
                U2T = ["u2T%d_%d" % (q_, k) for k in range(8)]
                for fc in range(NFC):
                    p2 = fc % 2
                    bg, bu = 3 + p2, 5 + p2
                    for kc in range(8):
                        S.op("pe", MM(pb[bg][:, 0:256], Wg[:, kc, fc * 128:(fc + 1) * 128], u2Tq[:, kc, :], start=(kc == 0), stop=(kc == 7)),
                             reads=["Wg", U2T[kc]], writes=["P%d" % bg])
                    for kc in range(8):
                        S.op("pe", MM(pb[bu][:, 0:256], Wu[:, kc, fc * 128:(fc + 1) * 128], u2Tq[:, kc, :], start=(kc == 0), stop=(kc == 7)),
                             reads=["Wu", U2T[kc]], writes=["P%d" % bu])
                    S.op("act", ACTF(sg[p2], pb[bg][:, 0:256], AF.Silu), reads=["P%d" % bg], writes=["sg%d" % p2])
                    S.op("dve", TT(actT[:, fc, :], sg[p2], pb[bu][:, 0:256], ALU.mult), reads=["sg%d" % p2, "P%d" % bu], writes=["actT%d" % fc])
                if g + 1 < NG:
                    prepB(g + 1)
                AT = ["actT%d" % fc for fc in range(NFC)]
                for j in range(2):
                    for half in range(2):
                        q = 1 + (2 * j + half) % 2
                        for fc in range(NFC):
                            S.op("pe", MM(pb[q], actT[:, fc, j * 128:(j + 1) * 128], Wd[:, fc, half * 512:(half + 1) * 512],
                                          start=(fc == 0), stop=(fc == NFC - 1)), reads=[AT[fc], "Wd"], writes=["P%d" % q])
                        sl = hbq[:, j, half * 512:(half + 1) * 512]
                        S.op("dve", TT(sl, sl, pb[q], ALU.add), reads=["P%d" % q, "hb%d" % q_], writes=["hb%d" % q_])
                    S.op("pool", MS(st[:, 4:5], 0.0), writes=["st2"])
                    S.op("act", ACTF(junk2, hbq[:, j, :], AF.Square, accum_out=st[:, 4:5]), reads=["hb%d" % q_], writes=["junk2", "st2"])
                    S.op("dve", TS(st[:, 5:6], st[:, 4:5], 1.0 / D, EPS, ALU.mult, ALU.add), reads=["st2"], writes=["st2"])
                    S.op("act", ACTF(st[:, 6:7], st[:, 5:6], AF.Ln), reads=["st2"], writes=["st2"])
                    S.op("act", ACTF(st[:, 7:8], st[:, 6:7], AF.Exp, scale=-0.5), reads=["st2"], writes=["st2"])
                    S.op("dve", STT(hbq[:, j, :], hbq[:, j, :], st[:, 7:8], finw, ALU.mult, ALU.mult), reads=["hb%d" % q_, "st2", "constB"], writes=["hb%d" % q_])
                S.op("sp", DMA(out_d[g * 256:(g + 1) * 256, :].rearrange("(t p) d -> p t d", p=128), hbq),
                     reads=["hb%d" % q_], writes=["outd"], dma="ost%d" % q_)

        fw = []
        for nm_ in ("ost0", "ost1", "h1st0", "h1st1", "T_dump"):
            if nm_ in S.dsem:
                fw.append(nm_)
        with nc.Block() as block:
            S.emit(block, final_waits=fw)
    return nc


def make_consts():
    p = np.arange(128)[:, None]
    f = np.arange(128)[None, :]
    identf = (p == f).astype(np.float32)
    tri = (p <= f).astype(np.float32)
    ones = np.ones((128, 128), np.float32)
    half = np.broadcast_to((p < 64), (128, 128)).astype(np.float32)
    posa = np.where(f < p, 0.0, 30000.0).astype(np.float32)
    negd = np.where(p <= f, 0.0, -30000.0).astype(np.float32)
    spare = np.zeros((128, 128), np.float32)
    cf = np.concatenate([identf, tri, ones, half, posa, negd, spare], axis=1)
    identb = identf.astype(ml_dtypes.bfloat16)
    mask01 = (p <= f).astype(np.float32).astype(ml_dtypes.bfloat16)
    cb = np.concatenate([identb, mask01], axis=1)
    return np.ascontiguousarray(cf), np.ascontiguousarray(cb)


def make_in_maps(x, meta_tokens, attn_norm_w, w_in, conv_w, a_log, dt_bias, gdn_norm_w, fgate_b,
                 w_out, ffn_norm_w, w_gate, w_up, w_down, final_norm_w):
    f = lambda a: np.ascontiguousarray(np.asarray(a, dtype=np.float32))
    cf, cb = make_consts()
    convw = f(np.asarray(conv_w)[0].reshape(4, 12, 128).transpose(2, 1, 0).reshape(128, 48))
    anw = f(np.asarray(attn_norm_w)[0].reshape(8, 128).T)
    fnw = f(np.asarray(ffn_norm_w)[0].reshape(8, 128).T)
    finw = f(np.broadcast_to(np.asarray(final_norm_w)[None, :], (128, D)))
    gnw = f(np.broadcast_to(np.tile(np.asarray(gdn_norm_w)[0], 4)[None, :], (128, 512)))
    raw12 = f(np.broadcast_to(np.concatenate([np.asarray(dt_bias)[0], np.asarray(fgate_b)[0]])[None, :], (128, 12)))
    sgn12 = f(np.broadcast_to(np.array([1.0] * 4 + [-1.0] * 8, np.float32)[None, :], (128, 12)))
    alog = f(np.broadcast_to(np.asarray(a_log)[0][None, :], (128, 4)))
    shared = dict(meta=f(meta_tokens), w_in=f(np.asarray(w_in)[0]), w_out=f(np.asarray(w_out)[0]),
                  w_gate=f(np.asarray(w_gate)[0]), w_up=f(np.asarray(w_up)[0]), w_down=f(np.asarray(w_down)[0]),
                  cf32=cf, cbf16=cb, convw=convw, anw=anw, fnw=fnw, finw=finw, gnw=gnw, raw12=raw12, sgn12=sgn12, alog=alog)
    x = np.asarray(x, dtype=np.float32)
    maps = []
    for b in range(x.shape[0]):
        m = dict(shared)
        m["x"] = np.ascontiguousarray(x[b])
        maps.append(m)
    return maps


def kernel(**inputs):
    x = np.asarray(inputs["x"])
    B, L, _ = x.shape
    NR = L // 128
    maps = make_in_maps(**inputs)
    nc = build_nc(NR)
    res = run_bass_kernel_spmd(nc, maps, core_ids=list(range(B)))
    out = np.stack([np.asarray(res.results[b]["out"]).reshape(L, D) for b in range(B)], axis=0)
    return out.astype(np.float32)
```

```python
import numpy as np
import ml_dtypes
from contextlib import ExitStack
import concourse.bass as bass
import concourse.mybir as mybir
from concourse.bass_utils import run_bass_kernel_spmd

F32 = mybir.dt.float32
BF16 = mybir.dt.bfloat16
AF = mybir.ActivationFunctionType
ALU = mybir.AluOpType

D = 1024
DIN = 3600
DFF = 2816
NFC = DFF // 128
EPS = 1e-6
NCORES = 8
SEQ = 4096


class Sched:
    ENG = ("pe", "act", "dve", "pool", "sp")

    def __init__(self, nc, es):
        self.nc = nc
        self.es = es
        self.ops = {e: [] for e in self.ENG}
        self.last_w = {}
        self.readers = {}
        self.sems = {e: es.enter_context(nc.semaphore("s_" + e)) for e in self.ENG}
        self.dsem = {}
        self.dcount = {}
        self.dlast = {}
        self.barrier_deps = []

    def op(self, eng, fn, reads=(), writes=(), dma=None):
        o = dict(eng=eng, fn=fn, deps=[], sig=False, dma=dma)
        for w in self.barrier_deps:
            self._dep(o, w)
        for r in reads:
            w = self.last_w.get(r)
            if w is not None:
                self._dep(o, w)
            if len(r) == 2 and r[0] == "P" and r[1].isdigit():
                for rd in self.readers.get(r, ()):
                    if rd["eng"] != eng:
                        self._dep(o, rd)
        for r in writes:
            w = self.last_w.get(r)
            if w is not None:
                self._dep(o, w)
            for rd in self.readers.get(r, ()):
                self._dep(o, rd)
        for r in reads:
            self.readers.setdefault(r, []).append(o)
        for r in writes:
            self.last_w[r] = o
            self.readers[r] = []
        if dma is not None:
            if dma not in self.dsem:
                self.dsem[dma] = self.es.enter_context(self.nc.semaphore("d_" + dma))
                self.dcount[dma] = 0
            self.dcount[dma] += 1
            o["dval"] = 16 * self.dcount[dma]
            self.dlast[dma] = o
        self.ops[eng].append(o)
        return o

    def _dep(self, o, w):
        if w is o:
            return
        if w["dma"] is None and o["dma"] is None and w["eng"] == "pe" and o["eng"] == "pe":
            return
        if w["dma"] is not None and o["dma"] == w["dma"] and w["dma"].startswith("T_"):
            return
        o["deps"].append(w)
        if w["dma"] is None:
            w["sig"] = True

    def barrier(self):
        deps = []
        for e in self.ENG:
            if self.ops[e]:
                deps.append(self.ops[e][-1])
        for d in self.dlast.values():
            deps.append(d)
        for w in deps:
            if w["dma"] is None:
                w["sig"] = True
        self.barrier_deps = deps

    def emit(self, block, final_waits=()):
        for e in self.ENG:
            c = 0
            for o in self.ops[e]:
                if o["dma"] is None and o["sig"]:
                    c += 1
                    o["sval"] = c
        S = self

        def run(eng_name):
            def body(eng):
                waited = {}
                for o in S.ops[eng_name]:
                    need = {}
                    for w in o["deps"]:
                        if w["dma"] is not None:
                            nm = w["dma"]
                            sem = S.dsem[nm]
                            val = 16 * S.dcount[nm] if nm.startswith("T_") else w["dval"]
                        else:
                            sem = S.sems[w["eng"]]
                            val = w["sval"]
                        k = sem.num
                        if k not in need or need[k][1] < val:
                            need[k] = (sem, val)
                    for k, (sem, val) in need.items():
                        if waited.get(k, 0) >= val:
                            continue
                        eng.wait_ge(sem, val)
                        waited[k] = val
                    ins = o["fn"](eng)
                    if o["dma"] is not None:
                        ins.then_inc(S.dsem[o["dma"]], 16)
                    elif o["sig"]:
                        ins.then_inc(S.sems[eng_name], 1)
                if eng_name == "sp":
                    for nm in final_waits:
                        eng.wait_ge(S.dsem[nm], 16 * S.dcount[nm])
            return body

        block.tensor(run("pe"))
        block.scalar(run("act"))
        block.vector(run("dve"))
        block.gpsimd(run("pool"))
        block.sync(run("sp"))


class Arena:
    def __init__(self, handle, nwords):
        self.h = handle
        self.n = nwords
        self.off = 0

    def alloc(self, free_shape, dtype):
        n = int(np.prod(free_shape))
        if dtype == BF16:
            words = (n + 1) // 2
        else:
            words = n
        assert self.off + words <= self.n, ("arena overflow", self.off, words, self.n)
        ap = self.h[:, self.off:self.off + words]
        self.off += words
        if dtype == BF16:
            ap = ap.bitcast(BF16)[:, 0:n]
        if len(free_shape) == 2:
            ap = ap.rearrange("p (a b) -> p a b", a=free_shape[0])
        elif len(free_shape) == 3:
            ap = ap.rearrange("p (a b c) -> p a b c", a=free_shape[0], b=free_shape[1])
        return ap


def MM(out, lhsT, rhs, start=True, stop=True):
    return lambda e: e.matmul(out, lhsT=lhsT, rhs=rhs, start=start, stop=stop)


def TR(out, in_, ident):
    return lambda e: e.transpose(out, in_, ident)


def ACTF(out, in_, func, bias=None, scale=None, accum_out=None):
    kw = {}
    if bias is not None:
        kw["bias"] = bias
    if scale is not None:
        kw["scale"] = scale
    if accum_out is not None:
        kw["accum_out"] = accum_out
    return lambda e: e.activation(out=out, in_=in_, func=func, **kw)


def AMUL(out, in_, mul):
    return lambda e: e.mul(out, in_, mul)


def TT(out, in0, in1, op):
    return lambda e: e.tensor_tensor(out=out, in0=in0, in1=in1, op=op)


def TS(out, in0, s1, s2, op0, op1=None):
    if op1 is None:
        return lambda e: e.tensor_scalar(out=out, in0=in0, scalar1=s1, scalar2=None, op0=op0)
    return lambda e: e.tensor_scalar(out=out, in0=in0, scalar1=s1, scalar2=s2, op0=op0, op1=op1)


def STT(out, in0, scalar, in1, op0, op1):
    return lambda e: e.scalar_tensor_tensor(out=out, in0=in0, scalar=scalar, in1=in1, op0=op0, op1=op1)


def CP(out, in_):
    return lambda e: e.tensor_copy(out=out, in_=in_)


def RCP(out, in_):
    return lambda e: e.reciprocal(out=out, in_=in_)


def MS(ap, val):
    return lambda e: e.memset(ap, val)


def DMA(out, in_):
    return lambda e: e.dma_start(out=out, in_=in_)


SRC = dict(gq=0, gk=512, gv=1024, gz=1536, gb=2048, ga=2052, fq=2056, fk=2568, fv=3080, ff=3592)
DST = dict(gq=0, gk=512, gv=1024, fq=1536, fk=2048, gz=2560, fv=3072, gb=3584, ga=3588, ff=3592)
WID = dict(gq=512, gk=512, gv=512, gz=512, gb=4, ga=4, fq=512, fk=512, fv=512, ff=8)


def build_nc(NR, phases="AB", dbg=False, stop=None, dumps=()):
    NT = NR + 1
    T = 16 + 128 * NR
    nc = bass.Bass("TRN2", target_bir_lowering=False)

    def dram(name, shape, dt=F32, kind="ExternalInput"):
        return nc.dram_tensor(name, list(shape), dt, kind=kind).ap()

    x_d = dram("x", [NR * 128, D])
    meta_d = dram("meta", [16, D])
    win_d = dram("w_in", [D, DIN])
    wout_d = dram("w_out", [D, D])
    wg_d = dram("w_gate", [D, DFF])
    wu_d = dram("w_up", [D, DFF])
    wd_d = dram("w_down", [DFF, D])
    cf_d = dram("cf32", [128, 7 * 128])
    cb_d = dram("cbf16", [128, 2 * 128], BF16)
    convw_d = dram("convw", [128, 48])
    anw_d = dram("anw", [128, 8])
    fnw_d = dram("fnw", [128, 8])
    finw_d = dram("finw", [128, D])
    gnw_d = dram("gnw", [128, 512])
    raw12_d = dram("raw12", [128, 12])
    sgn12_d = dram("sgn12", [128, 12])
    alog_d = dram("alog", [128, 4])
    out_d = dram("out", [NR * 128, D], kind="ExternalOutput")
    h1_d = dram("h1s", [NR * 128, D], kind="ExternalOutput" if dbg else "Internal")

    es = ExitStack()
    with es:
        NW = 53000
        arena_h = es.enter_context(nc.sbuf_tensor("arena", [128, NW], F32))
        AR = Arena(arena_h, NW)
        pb = [es.enter_context(nc.psum_tensor("pb%d" % i, [128, 512], F32))[:] for i in range(8)]
        pb0b = pb[0].bitcast(BF16)
        pb1b = pb[1].bitcast(BF16)
        S = Sched(nc, es)

        cf = AR.alloc([7, 128], F32)
        identf, TRI, ONES, HALF, POSA, NEGD = (cf[:, k, :] for k in range(6))
        cb = AR.alloc([2, 128], BF16)
        identb, MASK01 = cb[:, 0, :], cb[:, 1, :]
        S.op("sp", DMA(cf, cf_d.rearrange("p (a b) -> p a b", a=7)), writes=["const"], dma="T_c")
        S.op("sp", DMA(cb, cb_d.rearrange("p (a b) -> p a b", a=2)), writes=["const"], dma="T_c")
        common_mark = AR.off

        class StopBuild(Exception):
            pass

        def checkpoint(i, name, env):
            if stop is None or stop != (i, name):
                return
            for dn, fn_, reads in dumps:
                ap = fn_(env)
                d = nc.dram_tensor("dump_" + dn, list(ap.shape), ap.dtype, kind="ExternalOutput").ap()
                S.op("sp", DMA(d, ap), reads=reads, writes=["dump_" + dn], dma="T_dump")
            raise StopBuild()

        def small_load(dst, src):
            S.op("sp", DMA(dst, src), writes=["const"], dma="T_c")

        ev = [0]

        def evac_copy(out, in_, reads, writes, scale=None, eng=None):
            ev[0] += 1
            e = eng if eng is not None else ("act" if ev[0] % 2 else "dve")
            if e == "act":
                fn = ACTF(out, in_, AF.Copy) if scale is None else AMUL(out, in_, scale)
            else:
                fn = CP(out, in_) if scale is None else TS(out, in_, scale, None, ALU.mult)
            S.op(e, fn, reads=reads, writes=writes)

        UT = ["uT%d" % k for k in range(8)]
        UC = ["uc%d" % k for k in range(8)]

        if "A" in phases:
            Win = AR.alloc([8, DIN], BF16)
            Wout = AR.alloc([8, D], BF16)
            KT = AR.alloc([4, T], BF16)
            V = AR.alloc([NT, 8, 65], BF16)
            CC = AR.alloc([NT, 8], F32)
            biasTB = [AR.alloc([NT, 8], F32) for _ in range(2)]
            convw = AR.alloc([12, 4], F32)
            anw = AR.alloc([8], F32)
            gnw = AR.alloc([4, 128], F32)
            raw12 = AR.alloc([12], F32)
            sgn12 = AR.alloc([12], F32)
            bias12 = AR.alloc([12], F32)
            mul12 = AR.alloc([12], F32)
            alog = AR.alloc([4], F32)
            accf = AR.alloc([8], F32)
            xtB = [AR.alloc([D], F32) for _ in range(2)]
            uB = [AR.alloc([D], BF16) for _ in range(2)]
            uTB = [AR.alloc([8, 128], BF16) for _ in range(2)]
            pre = AR.alloc([12, 131], F32)
            cv = AR.alloc([12, 128], F32)
            QTg = AR.alloc([4, 128], BF16)
            KTg = AR.alloc([4, 128], BF16)
            VTg = AR.alloc([4, 128], BF16)
            QTfB = [AR.alloc([4, 128], BF16) for _ in range(2)]
            zsB = [AR.alloc([4, 128], F32) for _ in range(2)]
            sm = AR.alloc([16], F32)
            gtB = [AR.alloc([4, 12], F32) for _ in range(2)]
            sm4B = [AR.alloc([9, 4], F32) for _ in range(2)]
            ssq = AR.alloc([4], F32)
            so = AR.alloc([8], F32)
            crefB = [AR.alloc([8], F32) for _ in range(2)]
            rd = AR.alloc([4], F32)
            pgsB = [AR.alloc([24], F32) for _ in range(2)]
            junkb = AR.alloc([128], BF16)
            Bm4 = AR.alloc([4, 128], F32)
            Bm = [Bm4[:, q_, :] for q_ in range(4)]
            BTm4 = AR.alloc([4, 128], F32)
            BTm = [BTm4[:, q_, :] for q_ in range(4)]
            PTm4 = AR.alloc([4, 128], F32)
            PTm = [PTm4[:, q_, :] for q_ in range(4)]
            TTb4 = AR.alloc([4, 128], BF16)
            TTb = [TTb4[:, q_, :] for q_ in range(4)]
            kw = [AR.alloc([128], BF16) for _ in range(4)]
            kd = [AR.alloc([128], BF16) for _ in range(4)]
            vb = [AR.alloc([128], BF16) for _ in range(4)]
            wT4 = AR.alloc([4, 128], BF16)
            wT = [wT4[:, q_, :] for q_ in range(4)]
            vn4 = AR.alloc([4, 128], BF16)
            vn = [vn4[:, q_, :] for q_ in range(4)]
            QgT4 = AR.alloc([4, 128], BF16)
            QgT = [QgT4[:, q_, :] for q_ in range(4)]
            qkmT4 = AR.alloc([4, 128], BF16)
            qkmT = [qkmT4[:, q_, :] for q_ in range(4)]
            Sbf4 = AR.alloc([4, 128], BF16)
            Sbf = [Sbf4[:, q_, :] for q_ in range(4)]
            usb4 = AR.alloc([4, 128], F32)
            usb = [usb4[:, q_, :] for q_ in range(4)]
            Sst4 = AR.alloc([4, 128], F32)
            Sst = [Sst4[:, q_, :] for q_ in range(4)]
            pT = [[AR.alloc([128], BF16) for _ in range(4)] for _ in range(2)]
            Gb, DA4, DA, DT4, DT, eg4, eg = Bm, PTm4, PTm, usb4, usb, BTm4, BTm
            print("phase A arena words used", AR.off, "of", NW)

            for nm in ("gq", "gk", "gv", "fq", "fk", "gz", "fv", "gb", "ga", "ff"):
                s0, d0, wd = SRC[nm], DST[nm], WID[nm]
                S.op("pool", DMA(Win[:, :, d0:d0 + wd],
                                 win_d.rearrange("(k p) c -> p k c", p=128)[:, :, s0:s0 + wd]),
                     writes=["Win_" + nm], dma="T_w_" + nm)
            S.op("pool", DMA(Wout, wout_d.rearrange("(k p) c -> p k c", p=128)), writes=["Wout"], dma="T_wo")
            small_load(convw, convw_d.rearrange("p (a b) -> p a b", a=12))
            small_load(anw, anw_d)
            small_load(gnw, gnw_d.rearrange("p (a b) -> p a b", a=4))
            small_load(raw12, raw12_d)
            small_load(sgn12, sgn12_d)
            small_load(alog, alog_d)
            S.op("pool", MS(V[:, :, :, 64:65], 1.0), writes=["V%d" % t_ for t_ in range(NT)])
            S.op("pool", MS(pre, 0.0), writes=["pre"])
            S.op("pool", MS(accf, 0.0), writes=["accf"])
            S.op("pool", MS(CC, 0.0), writes=["CC"])
            for h in range(4):
                S.op("pool", MS(Sst[h], 0.0), writes=["S%d" % h])
                S.op("pool", MS(Sbf[h], 0.0), writes=["Sbf%d" % h])
            S.op("dve", TT(bias12, raw12, sgn12, ALU.mult), reads=["const"], writes=["g12"])
            S.op("act", ACTF(mul12[:, 0:4], alog, AF.Exp), reads=["const"], writes=["m12a"])
            S.op("dve", TS(mul12[:, 0:4], mul12[:, 0:4], -1.0, None, ALU.mult), reads=["m12a"], writes=["m12"])
            S.op("dve", MS(mul12[:, 4:12], -1.0), writes=["m12b"])

            def pf_gen(i):
                n = 16 if i == 0 else 128
                pos = 0 if i == 0 else 16 + 128 * (i - 1)
                p_ = i % 2
                xt, u, uT, QTf, zs = xtB[p_], uB[p_], uTB[p_], QTfB[p_], zsB[p_]
                XT, QTFN, ZSN = "xt%d" % p_, "QTf%d" % p_, "zs%d" % p_
                UC = ["uc%d_%d" % (p_, k) for k in range(8)]
                UT = ["uT%d_%d" % (p_, k) for k in range(8)]
                sm4, gt, pgs, cref, biasT = sm4B[p_], gtB[p_], pgsB[p_], crefB[p_], biasTB[p_]
                G_ = "_%d" % p_
                gcol = gt[:, 3, 0:4]
                lfcol = gt[:, 3, 4:12]
                beta, nbeta, ngc, egc, bege, egl, ekd, gtmp = (sm4[:, k, :] for k in range(1, 9))
                src = meta_d if i == 0 else x_d[(i - 1) * 128:i * 128, :]
                S.op("sp", DMA(xt[:n], src), writes=[XT], dma="xld%d" % p_)
                yield
                S.op("pool", MS(ssq[:, 0:1], 0.0), writes=["ssq"])
                S.op("act", ACTF(u[:n], xt[:n], AF.Square, accum_out=ssq[:n, 0:1]), reads=[XT], writes=UC + ["ssq"])
                S.op("dve", TS(ssq[:n, 1:2], ssq[:n, 0:1], 1.0 / D, EPS, ALU.mult, ALU.add), reads=["ssq"], writes=["ssq1"])
                S.op("act", ACTF(ssq[:n, 2:3], ssq[:n, 1:2], AF.Ln), reads=["ssq1"], writes=["ssq2"])
                S.op("act", ACTF(ssq[:n, 3:4], ssq[:n, 2:3], AF.Exp, scale=-0.5), reads=["ssq2"], writes=["rstd"])
                S.op("dve", TS(u[:n], xt[:n], ssq[:n, 3:4], None, ALU.mult), reads=[XT, "rstd"], writes=UC)
                yield
                for kc in range(8):
                    S.op("pe", TR(pb1b[:, kc * 128:kc * 128 + n], u[:n, kc * 128:(kc + 1) * 128], identb[:n, :n]),
                         reads=[UC[kc], "const"], writes=["P1"])
                e3 = "act" if i % 2 else "dve"
                for kc in range(8):
                    evac_copy(uT[:, kc, :n], pb1b[:, kc * 128:kc * 128 + n], ["P1", "const"], [UT[kc]],
                              scale=anw[:, kc:kc + 1], eng=e3)
                yield
                for g in range(5):
                    bi = 2 - g % 2
                    bank, bn = pb[bi], "P%d" % bi
                    for c4 in range(4):
                        c = 4 * g + c4
                        for kc in range(8):
                            S.op("pe", MM(bank[:, c4 * 128:c4 * 128 + n], Win[:, kc, c * 128:(c + 1) * 128],
                                          uT[:, kc, :n], start=(kc == 0), stop=(kc == 7)),
                                 reads=[UT[kc], "Win_" + ("gq", "gk", "gv", "fq", "fk")[g]], writes=[bn])
                    srcp = bank.rearrange("p (c t) -> p c t", c=4)[:, :, :n]
                    yield
                    if g < 3:
                        evac_copy(pre[:, 4 * g:4 * g + 4, 3:3 + n], srcp, [bn], ["pre"])
                    elif g == 3:
                        evac_copy(QTf[:, :, :n], srcp, [bn], [QTFN], scale=0.125)
                    else:
                        evac_copy(KT[:, :, pos:pos + n], srcp, [bn], ["KT%d" % i])
                yield
                if i > 0:
                    for kc in range(8):
                        S.op("pe", MM(pb[1][:n, :], uT[:, kc, :n], Win[:, kc, 2560:3072], start=(kc == 0), stop=(kc == 7)),
                             reads=[UT[kc], "Win_gz"], writes=["P1"])
                    S.op("act", ACTF(zs[:n].rearrange("p a b -> p (a b)"), pb[1][:n, :], AF.Silu), reads=["P1"], writes=[ZSN])
                    S.op("pool", TT(zs[:n], zs[:n], gnw[:n], ALU.mult), reads=[ZSN, "const"], writes=[ZSN])
                for kc in range(8):
                    S.op("pe", MM(pb[2][:n, :], uT[:, kc, :n], Win[:, kc, 3072:3584], start=(kc == 0), stop=(kc == 7)),
                         reads=[UT[kc], "Win_fv"], writes=["P2"])
                evac_copy(V[:n, i, :, 0:64], pb[2][:n, :].rearrange("p (h d) -> p h d", h=8), ["P2"], ["V%d" % i])
                for kc in range(8):
                    S.op("pe", MM(pb[1][:n, 0:16], uT[:, kc, :n], Win[:, kc, 3584:3600], start=(kc == 0), stop=(kc == 7)),
                         reads=[UT[kc], "Win_gb", "Win_ga", "Win_ff"], writes=["P1"])
                S.op("dve", CP(sm[:n], pb[1][:n, 0:16]), reads=["P1"], writes=["sm"])
                yield
                S.op("dve", TT(gt[:n, 0, :], sm[:n, 4:16], sgn12[:n], ALU.mult), reads=["sm", "const"], writes=["gt0" + G_])
                S.op("dve", TT(gt[:n, 0, :], gt[:n, 0, :], bias12[:n], ALU.add), reads=["gt0" + G_, "g12"], writes=["gt0" + G_])
                S.op("act", ACTF(gt[:n, 1, :], gt[:n, 0, :], AF.Exp), reads=["gt0" + G_], writes=["gt1" + G_])
                S.op("dve", TS(gt[:n, 1, :], gt[:n, 1, :], 1.0, None, ALU.add), reads=["gt1" + G_], writes=["gt1" + G_])
                S.op("act", ACTF(gt[:n, 2, :], gt[:n, 1, :], AF.Ln), reads=["gt1" + G_], writes=["gt2" + G_])
                S.op("dve", TT(gt[:n, 3, :], gt[:n, 2, :], mul12[:n], ALU.mult), reads=["gt2" + G_, "m12", "m12b"], writes=["GL" + G_])
                S.op("act", ACTF(sm4[:n, 0, :], sm[:n, 0:4], AF.Exp, scale=-1.0), reads=["sm"], writes=["b0" + G_])
                S.op("dve", TS(sm4[:n, 0, :], sm4[:n, 0, :], 1.0, None, ALU.add), reads=["b0" + G_], writes=["b0" + G_])
                S.op("dve", RCP(sm4[:n, 1, :], sm4[:n, 0, :]), reads=["b0" + G_], writes=["beta" + G_])
                S.op("dve", TS(sm4[:n, 2, :], sm4[:n, 1, :], -1.0, None, ALU.mult), reads=["beta" + G_], writes=["nbeta" + G_])
                pg = pb[2]
                S.op("pe", MM(pg[:n, 0:4], TRI[:n, :n], gcol[:n]), reads=["GL" + G_, "const"], writes=["P2"])
                S.op("pe", MM(pg[:, 4:8], ONES[:n, :], gcol[:n]), reads=["GL" + G_, "const"], writes=["P2"])
                S.op("pe", MM(pg[:n, 8:16], TRI[:n, :n], lfcol[:n], start=True, stop=False), reads=["GL" + G_, "const"], writes=["P2"])
                S.op("pe", MM(pg[:n, 8:16], ONES[:, :n], accf, start=False, stop=True), reads=["accf", "const"], writes=["P2"])
                hsel = ONES if i == 0 else HALF
                S.op("pe", MM(pg[:, 16:24], hsel[:n, :], lfcol[:n], start=True, stop=False), reads=["GL" + G_, "const"], writes=["P2"])
                S.op("pe", MM(pg[:, 16:24], ONES, accf, start=False, stop=True), reads=["accf", "const"], writes=["P2"])
                S.op("dve", CP(pgs[:, 4:8], pg[:, 4:8]), reads=["P2"], writes=["pgs" + G_])
                S.op("dve", CP(pgs[:, 16:24], pg[:, 16:24]), reads=["P2"], writes=["pgs" + G_])
                S.op("dve", CP(pgs[:n, 0:4], pg[:n, 0:4]), reads=["P2"], writes=["pgs" + G_])
                S.op("dve", CP(pgs[:n, 8:16], pg[:n, 8:16]), reads=["P2"], writes=["pgs" + G_])
                S.op("dve", TS(ngc[:n], pgs[:n, 0:4], -1.0, None, ALU.mult), reads=["pgs" + G_], writes=["ngc" + G_])
                S.op("act", ACTF(egc[:n], pgs[:n, 0:4], AF.Exp), reads=["pgs" + G_], writes=["egc" + G_])
                S.op("dve", TT(bege[:n], beta[:n], egc[:n], ALU.mult), reads=["beta" + G_, "egc" + G_], writes=["bege" + G_])
                S.op("act", ACTF(egl, pgs[:, 4:8], AF.Exp), reads=["pgs" + G_], writes=["egl" + G_])
                S.op("dve", TT(gtmp[:n], pgs[:n, 4:8], ngc[:n], ALU.add), reads=["pgs" + G_, "ngc" + G_], writes=["gtmp" + G_])
                S.op("act", ACTF(ekd[:n], gtmp[:n], AF.Exp), reads=["gtmp" + G_], writes=["ekd" + G_])
                S.op("pool", CP(CC[:n, i, :], pgs[:n, 8:16]), reads=["pgs" + G_], writes=["CC"])
                S.op("pool", CP(cref, pgs[:, 16:24]), reads=["pgs" + G_], writes=["cref" + G_])
                S.op("pool", TT(accf[:n], accf[:n], lfcol[:n], ALU.add), reads=["GL" + G_, "accf"], writes=["accf"])
                if i > 0:
                    for h in range(8):
                        S.op("pool", TS(biasT[:, 0:i + 1, h], CC[:, 0:i + 1, h], -1.0, cref[:, h:h + 1], ALU.mult, ALU.add),
                             reads=["CC", "cref" + G_], writes=["biasT" + G_])

                yield
                for c in range(12):
                    yield
                    e = "dve"
                    S.op(e, TS(cv[:, c, :n], pre[:, c, 3:3 + n], convw[:, c, 3:4], None, ALU.mult),
                         reads=["pre", "const"], writes=["cv%d" % c])
                    for k in range(3):
                        if e == "dve":
                            S.op(e, STT(cv[:, c, :n], pre[:, c, k:k + n], convw[:, c, k:k + 1], cv[:, c, :n], ALU.mult, ALU.add),
                                 reads=["pre", "const", "cv%d" % c], writes=["cv%d" % c])
                        else:
                            S.op(e, TS(ctmp[:, :n], pre[:, c, k:k + n], convw[:, c, k:k + 1], None, ALU.mult),
                                 reads=["pre", "const"], writes=["ctmp"])
                            S.op(e, TT(cv[:, c, :n], cv[:, c, :n], ctmp[:, :n], ALU.add),
                                 reads=["ctmp", "cv%d" % c], writes=["cv%d" % c])
                cvall = ["cv%d" % c for c in range(12)]
                S.op("dve", CP(pre[:, :, 0:3], pre[:, :, n:n + 3]), reads=["pre"], writes=["pre"])
                S.op("act", ACTF(cv[:, 0:8, :n], cv[:, 0:8, :n], AF.Silu), reads=cvall, writes=cvall + ["cvs"])
                S.op("act", ACTF(VTg[:, :, :n], cv[:, 8:12, :n], AF.Silu), reads=cvall, writes=["VTg"])
                sq = pre[:, 0:8, 3:3 + n]
                S.op("dve", TT(sq, cv[:, 0:8, :n], cv[:, 0:8, :n], ALU.mult), reads=["cvs"], writes=["pre"])
                for _ in range(6):
                    yield
                for g2 in range(2):
                    bank, bn = pb[1 + g2], "P%d" % (1 + g2)
                    for c4 in range(4):
                        S.op("pe", MM(bank[:, c4 * 128:c4 * 128 + n], ONES, pre[:, 4 * g2 + c4, 3:3 + n]),
                             reads=["pre", "const"], writes=[bn])
                yield
                for g2 in range(2):
                    bank, bn = pb[1 + g2], "P%d" % (1 + g2)
                    srcp = bank.rearrange("p (c t) -> p c t", c=4)[:, :, :n]
                    dst = pre[:, 4 * g2:4 * g2 + 4, 3:3 + n]
                    S.op("dve", TS(dst, srcp, EPS, None, ALU.add), reads=[bn], writes=["pre"])
                    S.op("act", ACTF(dst, dst, AF.Ln), reads=["pre"], writes=["pre"])
                    S.op("act", ACTF(dst, dst, AF.Exp, scale=-0.5), reads=["pre"], writes=["pre"])
                S.op("dve", STT(QTg[:, :, :n], cv[:, 0:4, :n], 128.0 ** -0.5, pre[:, 0:4, 3:3 + n], ALU.mult, ALU.mult),
                     reads=["cvs", "pre"], writes=["QTg"])
                S.op("dve", TT(KTg[:, :, :n], cv[:, 4:8, :n], pre[:, 4:8, 3:3 + n], ALU.mult), reads=["cvs", "pre"], writes=["KTg"])

            WIN_NAMES = ["Win_" + nm for nm in ("gq", "gk", "gv", "fq", "fk", "gz", "fv", "gb", "ga", "ff")]
            Wg_early = Win.rearrange("p k c -> p (k c)")[:, 0:8 * DFF].rearrange("p (k c) -> p k c", k=8)

            def tile_body(i):
                n = 16 if i == 0 else 128
                pos = 0 if i == 0 else 16 + 128 * (i - 1)
                if i == NT - 1 and "B" in phases and stop is None:
                    S.op("pool", DMA(Wg_early, wg_d.rearrange("(k p) c -> p k c", p=128)), writes=WIN_NAMES + ["Wg"], dma="T_w2")
                p_ = i % 2
                xt, u, uT, QTf, zs = xtB[p_], uB[p_], uTB[p_], QTfB[p_], zsB[p_]
                XT, QTFN, ZSN = "xt%d" % p_, "QTf%d" % p_, "zs%d" % p_
                UC = ["uc%d_%d" % (p_, k) for k in range(8)]
                UT = ["uT%d_%d" % (p_, k) for k in range(8)]
                sm4, gt, pgs, cref, biasT = sm4B[p_], gtB[p_], pgsB[p_], crefB[p_], biasTB[p_]
                G_ = "_%d" % p_
                gcol = gt[:, 3, 0:4]
                lfcol = gt[:, 3, 4:12]
                beta, nbeta, ngc, egc, bege, egl, ekd, gtmp = (sm4[:, k, :] for k in range(1, 9))
                def gdn_gen():
                    yield
                    for h in range(4):
                        S.op("pe", MM(pb[1][:n, h * 128:(h + 1) * 128], KTg[:, h, :n], identb), reads=["KTg", "const"], writes=["P1"])
                    yield
                    for h in range(4):
                        S.op("pe", MM(pb[2][:n, h * 128:(h + 1) * 128], VTg[:, h, :n], identb), reads=["VTg", "const"], writes=["P2"])
                    yield
                    for h in range(4):
                        S.op("dve", TS(kw[h][:n], pb[1][:n, h * 128:(h + 1) * 128], bege[:n, h:h + 1], None, ALU.mult),
                             reads=["P1", "bege" + G_], writes=["kw%d" % h])
                        S.op("dve", TS(kd[h][:n], pb[1][:n, h * 128:(h + 1) * 128], ekd[:n, h:h + 1], None, ALU.mult),
                             reads=["P1", "ekd" + G_], writes=["kd%d" % h])
                        S.op("act", AMUL(vb[h][:n], pb[2][:n, h * 128:(h + 1) * 128], beta[:n, h:h + 1]),
                             reads=["P2", "beta" + G_], writes=["vb%d" % h])
                    checkpoint(i, "prep", locals())
                    yield "PREP_DONE"
                    yield
                    for h in range(4):
                        S.op("dve", TS(Gb[h][:n], ONES[:n], gcol[:n, h:h + 1], None, ALU.mult), reads=["GL" + G_, "const"], writes=["B%d" % h])
                        S.op("pe", MM(pb[4][:, h * 128:h * 128 + n], Gb[h][:n], TRI[:n, :n]), reads=["B%d" % h, "const"], writes=["P4"])
                    yield
                    for h in range(4):
                        S.op("pe", MM(pb[5][:n, h * 128:h * 128 + n], KTg[:, h, :n], KTg[:, h, :n]), reads=["KTg"], writes=["P5"])
                    if i > 0:
                        for h in range(4):
                            S.op("pe", MM(pb[6][:n, h * 128:h * 128 + n], KTg[:, h, :n], QTg[:, h, :n]), reads=["KTg", "QTg"], writes=["P6"])
                    yield
                    S.op("dve", CP(eg4[:, :, :n], pb[4].rearrange("p (h c) -> p h c", h=4)[:, :, :n]), reads=["P4"], writes=["BT%d" % q for q in range(4)])
                    yield
                    for h in range(4):
                        gr = eg[h][:n, :n]
                        S.op("dve", STT(DA[h][:n, :n], gr, ngc[:n, h:h + 1], POSA[:n, :n], ALU.add, ALU.add),
                             reads=["BT%d" % h, "ngc" + G_, "const"], writes=["PT%d" % h])
                    S.op("act", ACTF(DA4[:n, :, :n], DA4[:n, :, :n], AF.Exp, scale=-1.0), reads=["PT%d" % q for q in range(4)], writes=["PT%d" % q for q in range(4)])
                    for h in range(4):
                        gr = eg[h][:n, :n]
                        S.op("dve", STT(DT[h][:n, :n], gr, ngc[:n, h:h + 1], NEGD[:n, :n], ALU.add, ALU.add),
                             reads=["BT%d" % h, "ngc" + G_, "const"], writes=["usb%d" % h])
                    S.op("act", ACTF(DT4[:n, :, :n], DT4[:n, :, :n], AF.Exp), reads=["usb%d" % q for q in range(4)], writes=["usb%d" % q for q in range(4)])
                    if i > 0:
                        S.op("act", ACTF(eg4[:, :, :n], eg4[:, :, :n], AF.Exp), reads=["BT%d" % q for q in range(4)], writes=["BT%d" % q for q in range(4)])
                        S.op("pool", TT(QgT4[:, :, :n], QTg[:, :, :n], eg4[:, :, :n], ALU.mult), reads=["QTg"] + ["BT%d" % q for q in range(4)],
                             writes=["QgT%d" % q for q in range(4)])
                    yield
                    for h in range(4):
                        S.op("dve", STT(Bm[h][:n, :n], pb[5][:n, h * 128:h * 128 + n], nbeta[:n, h:h + 1], DA[h][:n, :n], ALU.mult, ALU.mult),
                             reads=["P5", "nbeta" + G_, "PT%d" % h], writes=["B%d" % h])
                    if i > 0:
                        S.op("dve", TT(qkmT4[:n, :, :n], pb[6].rearrange("p (h c) -> p h c", h=4)[:n, :, :n], DT4[:n, :, :n], ALU.mult),
                             reads=["P6"] + ["usb%d" % q for q in range(4)], writes=["qkmT%d" % q for q in range(4)])
                    yield
                    for h in range(4):
                        S.op("pe", TR(pb[4][:n, h * 128:h * 128 + n], Bm[h][:n, :n], identf[:n, :n]), reads=["B%d" % h, "const"], writes=["P4"])
                    yield
                    S.op("dve", CP(BTm4[:n, :, :n], pb[4].rearrange("p (h c) -> p h c", h=4)[:n, :, :n]), reads=["P4"], writes=["BT%d" % q for q in range(4)])
                    for h in range(4):
                        S.op("dve", TT(PTm[h][:n, :n], BTm[h][:n, :n], identf[:n, :n], ALU.add), reads=["BT%d" % h, "const"], writes=["PT%d" % h])
                    checkpoint(i, "gdn7a", locals())
                    nst = 6 if i > 0 else 3
                    yield
                    for k in range(nst):
                        last = (k == nst - 1)
                        yield
                        for h in range(4):
                            S.op("pe", MM(pb[4][:n, h * 128:h * 128 + n], BTm[h][:n, :n], Bm[h][:n, :n]),
                                 reads=["BT%d" % h, "B%d" % h], writes=["P4"])
                        if not last:
                            for h in range(4):
                                S.op("pe", MM(pb[5][:n, h * 128:h * 128 + n], Bm[h][:n, :n], BTm[h][:n, :n]),
                                     reads=["BT%d" % h, "B%d" % h], writes=["P5"])
                        yield
                        for h in range(4):
                            S.op("dve", CP(Bm[h][:n, :n], pb[4][:n, h * 128:h * 128 + n]), reads=["P4"], writes=["B%d" % h])
                        if not last:
                            S.op("act", ACTF(BTm4[:n, :, :n], pb[5].rearrange("p (h c) -> p h c", h=4)[:n, :, :n], AF.Copy), reads=["P5"], writes=["BT%d" % q for q in range(4)])
                        yield
                        for h in range(4):
                            S.op("pe", MM(pb[6][:n, h * 128:h * 128 + n], Bm[h][:n, :n], PTm[h][:n, :n]),
                                 reads=["B%d" % h, "PT%d" % h], writes=["P6"])
                        yield
                        for h in range(4):
                            S.op("dve", TT(PTm[h][:n, :n], pb[6][:n, h * 128:h * 128 + n], PTm[h][:n, :n], ALU.add),
                                 reads=["P6", "PT%d" % h], writes=["PT%d" % h])
                    yield
                    S.op("dve", CP(TTb4[:n, :, :n], PTm4[:n, :, :n]), reads=["PT%d" % q for q in range(4)], writes=["TTb%d" % q for q in range(4)])
                    yield
                    for h in range(4):
                        S.op("pe", MM(pb[4][:n, h * 128:(h + 1) * 128], TTb[h][:n, :n], vb[h][:n, :]), reads=["TTb%d" % h, "vb%d" % h], writes=["P4"])
                    yield
                    for h in range(4):
                        S.op("pe", MM(pb[5][:, h * 128:h * 128 + n], kw[h][:n, :], TTb[h][:n, :n]), reads=["TTb%d" % h, "kw%d" % h], writes=["P5"])
                    yield
                    S.op("dve", CP(usb4[:n], pb[4].rearrange("p (h c) -> p h c", h=4)[:n]), reads=["P4"], writes=["usb%d" % q for q in range(4)])
                    yield
                    S.op("dve", CP(wT4[:, :, :n], pb[5].rearrange("p (h c) -> p h c", h=4)[:, :, :n]), reads=["P5"], writes=["wT%d" % q for q in range(4)])
                    checkpoint(i, "gdn7", locals())
                    yield
                    for h in range(4):
                        S.op("pe", MM(pb[6][:n, h * 128:(h + 1) * 128], wT[h][:, :n], Sbf[h]), reads=["wT%d" % h, "Sbf%d" % h], writes=["P6"])
                    yield
                    S.op("dve", TT(vn4[:n], usb4[:n], pb[6].rearrange("p (h c) -> p h c", h=4)[:n], ALU.subtract),
                         reads=["P6"] + ["usb%d" % q for q in range(4)], writes=["vn%d" % q for q in range(4)])
                    if i > 0:
                        for h in range(4):
                            S.op("pe", MM(pb[4][:n, h * 128:(h + 1) * 128], QgT[h][:, :n], Sbf[h], start=True, stop=False),
                                 reads=["QgT%d" % h, "Sbf%d" % h], writes=["P4"])
                            S.op("pe", MM(pb[4][:n, h * 128:(h + 1) * 128], qkmT[h][:n, :n], vn[h][:n, :], start=False, stop=True),
                                 reads=["qkmT%d" % h, "vn%d" % h], writes=["P4"])
                    yield
                    for h in range(4):
                        S.op("pe", MM(pb[5][:, h * 128:(h + 1) * 128], kd[h][:n, :], vn[h][:n, :]), reads=["kd%d" % h, "vn%d" % h], writes=["P5"])
                    if i > 0:
                        S.op("pool", MS(so[:, 0:4], 0.0), writes=["so"])
                        for h in range(4):
                            S.op("act", ACTF(junkb[:n, :], pb[4][:n, h * 128:(h + 1) * 128], AF.Square, accum_out=so[:n, h:h + 1]),
                                 reads=["P4"], writes=["so", "junkb"])
                        S.op("dve", TS(so[:n, 4:8], so[:n, 0:4], 1.0 / 128, EPS, ALU.mult, ALU.add), reads=["so"], writes=["so"])
                        S.op("act", ACTF(so[:n, 4:8], so[:n, 4:8], AF.Ln), reads=["so"], writes=["so"])
                        S.op("act", ACTF(so[:n, 4:8], so[:n, 4:8], AF.Exp, scale=-0.5), reads=["so"], writes=["so"])
                        for h in range(4):
                            S.op("act", AMUL(usb[h][:n, :], pb[4][:n, h * 128:(h + 1) * 128], so[:n, 4 + h:5 + h]),
                                 reads=["P4", "so"], writes=["usb%d" % h])
                    if i > 0:
                        S.op("dve", TT(u[:n, 0:512].rearrange("p (h c) -> p h c", h=4), usb4[:n], zs[:n], ALU.mult),
                             reads=["usb%d" % q for q in range(4)] + [ZSN], writes=UC[0:4])
                    yield
                    for h in range(4):
                        S.op("dve", STT(Sst[h], Sst[h], egl[:, h:h + 1], pb[5][:, h * 128:(h + 1) * 128], ALU.mult, ALU.add),
                             reads=["P5", "egl" + G_, "S%d" % h], writes=["S%d" % h])
                    S.op("pool", CP(Sbf4, Sst4), reads=["S%d" % q for q in range(4)], writes=["Sbf%d" % q for q in range(4)])
                    yield
                def attn_gen():
                    items = []
                    for h in range(8):
                        kts = list(range(i + 1))
                        for c0 in range(0, len(kts), 4):
                            items.append((h, kts[c0:c0 + 4]))

                    def scores(idx):
                        h, ks = items[idx]
                        hp, r0 = h // 2, 64 * (h % 2)
                        bi = (0, 7)[idx % 2]
                        st_ = idx % 2
                        for j, kt in enumerate(ks):
                            nk = 16 if kt == 0 else 128
                            kpos = 0 if kt == 0 else 16 + 128 * (kt - 1)
                            S.op("pe", MM(pb[bi][:nk, j * 128:(j + 1) * 128], KT[r0:r0 + 64, hp, kpos:kpos + nk], QTf[r0:r0 + 64, hp, :]),
                                 reads=["KT%d" % kt, QTFN], writes=["P%d" % bi])
                        for j, kt in enumerate(ks):
                            nk = 16 if kt == 0 else 128
                            S.op("act", ACTF(pT[st_][j][:nk, :], pb[bi][:nk, j * 128:(j + 1) * 128], AF.Exp, bias=biasT[:nk, kt, h:h + 1], scale=1.0),
                                 reads=["P%d" % bi, "biasT" + G_], writes=["pT%d_%d" % (st_, j)])
                            if kt == i:
                                S.op("dve", TT(pT[st_][j][:nk, :], pT[st_][j][:nk, :], MASK01[:nk, :], ALU.mult),
                                     reads=["pT%d_%d" % (st_, j), "const"], writes=["pT%d_%d" % (st_, j)])

                    def pv(idx):
                        h, ks = items[idx]
                        st_ = idx % 2
                        hh = h % 4
                        fb = 3
                        for j, kt in enumerate(ks):
                            nk = 16 if kt == 0 else 128
                            S.op("pe", MM(pb[fb][:, hh * 65:(hh + 1) * 65], pT[st_][j][:nk, :], V[:nk, kt, h, :], start=(kt == 0), stop=(kt == i)),
                                 reads=["pT%d_%d" % (st_, j), "V%d" % kt], writes=["P%d" % fb])
                        if ks[-1] == i and hh == 3:
                            fo3 = pb[fb][:, 0:260].rearrange("p (h d) -> p h d", h=4)
                            S.op("dve", RCP(rd, fo3[:, :, 64]), reads=["P%d" % fb], writes=["rd"])
                            for q in range(4):
                                hq = h - 3 + q
                                evac_copy(u[:, 512 + hq * 64:512 + (hq + 1) * 64], fo3[:, q, 0:64], ["P%d" % fb, "rd"], [UC[4 + hq // 2]],
                                          scale=rd[:, q:q + 1], eng="dve")

                    scores(0)
                    for idx in range(len(items)):
                        if idx + 1 < len(items):
                            scores(idx + 1)
                        pv(idx)
                        yield
                def run_streams(gens, late=None, first=None):
                    gens = list(gens)
                    prep_done = False
                    while gens:
                        for g_ in list(gens):
                            try:
                                v_ = next(g_)
                            except StopIteration:
                                gens.remove(g_)
                                if g_ is first:
                                    first = None
                                continue
                            if v_ == "PREP_DONE":
                                prep_done = True
                        if late is not None and prep_done and first is None:
                            gens.append(late)
                            late = None
                    if late is not None:
                        for _ in late:
                            pass
                nxt = pf_gen(i + 1) if i + 1 < NT else None
                if i == 0:
                    run_streams([gdn_gen()], late=nxt)
                    return
                def out_gen():
                    for kc in range(8):
                        S.op("pe", TR(pb0b[:, kc * 128:(kc + 1) * 128], u[:, kc * 128:(kc + 1) * 128], identb), reads=[UC[kc], "const"], writes=["P0"])
                    for kc in range(8):
                        evac_copy(uT[:, kc, :], pb0b[:, kc * 128:(kc + 1) * 128], ["P0"], [UT[kc]], eng=("dve" if i % 2 else "act"))
                    yield
                    for half in range(2):
                        bi_ = (7, 0)[half]
                        bank, bn = pb[bi_], "P%d" % bi_
                        for kc in range(8):
                            S.op("pe", MM(bank, uT[:, kc, :], Wout[:, kc, half * 512:(half + 1) * 512], start=(kc == 0), stop=(kc == 7)),
                                 reads=[UT[kc], "Wout"], writes=[bn])
                        S.op("dve", TT(xt[:, half * 512:(half + 1) * 512], xt[:, half * 512:(half + 1) * 512], bank, ALU.add),
                             reads=[bn, XT], writes=[XT])
                        if half == 0:
                            yield
                    S.op("sp", DMA(h1_d[(i - 1) * 128:i * 128, :], xt), reads=[XT], writes=["h1d"], dma="h1st%d" % p_)

                prev_out = pending_out[0]
                pending_out[0] = out_gen()
                run_streams([gdn_gen(), attn_gen()] + ([prev_out] if prev_out is not None else []), late=nxt, first=prev_out)

            pending_out = [None]
            try:
                for _ in pf_gen(0):
                    pass
                for i in range(NT):
                    tile_body(i)
                if pending_out[0] is not None:
                    for _ in pending_out[0]:
                        pass
            except StopBuild:
                pass

        if "B" in phases and stop is None:
            S.barrier()
            AR.off = common_mark
            Wg = AR.alloc([8, DFF], BF16)
            Wu = AR.alloc([8, DFF], BF16)
            Wd = AR.alloc([NFC, D], BF16)
            fnw = AR.alloc([8], F32)
            finw = AR.alloc([D], F32)
            hb = AR.alloc([2, D], F32)
            hb2 = AR.alloc([2, D], F32)
            u2T2 = AR.alloc([8, 256], BF16)
            junk2 = AR.alloc([D], BF16)
            u2 = AR.alloc([D], BF16)
            u2T = AR.alloc([8, 256], BF16)
            actT = AR.alloc([NFC, 256], BF16)
            sg = [AR.alloc([256], F32) for _ in range(2)]
            st = AR.alloc([8], F32)
            print("phase B arena words used", AR.off, "of", NW)
            if "A" not in phases:
                S.op("pool", DMA(Wg, wg_d.rearrange("(k p) c -> p k c", p=128)), writes=["Wg"], dma="T_w2")
            S.op("pool", DMA(Wu, wu_d.rearrange("(k p) c -> p k c", p=128)), writes=["Wu"], dma="T_w2")
            S.op("pool", DMA(Wd, wd_d.rearrange("(k p) c -> p k c", p=128)), writes=["Wd"], dma="T_w2")
            S.op("sp", DMA(fnw, fnw_d), writes=["constB"], dma="T_c2")
            S.op("sp", DMA(finw, finw_d), writes=["constB"], dma="T_c2")
            NG = NR // 2
            hbB = [hb, hb2]
            u2TB = [u2T, u2T2]

            def prepB(g):
                q_ = g % 2
                hbq, u2Tq = hbB[q_], u2TB[q_]
                S.op("sp", DMA(hbq, h1_d[g * 256:(g + 1) * 256, :].rearrange("(t p) d -> p t d", p=128)),
                     reads=["h1d"], writes=["hb%d" % q_], dma="hld%d" % q_)
                for j in range(2):
                    S.op("pool", MS(st[:, 0:1], 0.0), writes=["st"])
                    S.op("act", ACTF(u2, hbq[:, j, :], AF.Square, accum_out=st[:, 0:1]), reads=["hb%d" % q_], writes=["u2", "st"])
                    S.op("dve", TS(st[:, 1:2], st[:, 0:1], 1.0 / D, EPS, ALU.mult, ALU.add), reads=["st"], writes=["st"])
                    S.op("act", ACTF(st[:, 2:3], st[:, 1:2], AF.Ln), reads=["st"], writes=["st"])
                    S.op("act", ACTF(st[:, 3:4], st[:, 2:3], AF.Exp, scale=-0.5), reads=["st"], writes=["st"])
                    S.op("dve", TS(u2, hbq[:, j, :], st[:, 3:4], None, ALU.mult), reads=["hb%d" % q_, "st"], writes=["u2"])
                    for kc in range(8):
                        S.op("pe", TR(pb0b[:, kc * 128:(kc + 1) * 128], u2[:, kc * 128:(kc + 1) * 128], identb),
                             reads=["u2", "const"], writes=["P0"])
                    for kc in range(8):
                        evac_copy(u2Tq[:, kc, j * 128:(j + 1) * 128], pb0b[:, kc * 128:(kc + 1) * 128], ["P0", "constB"],
                                  ["u2T%d_%d" % (q_, kc)], scale=fnw[:, kc:kc + 1], eng=("act" if j else "dve"))

            prepB(0)
            for g in range(NG):
                q_ = g % 2
                hbq, u2Tq = hbB[q_], u2TB[q_]
                U2T = ["u2T%d_%d" % (q_, k) for k in range(8)]
                for fc in range(NFC):
                    p2 = fc % 2
                    bg, bu = 3 + p2, 5 + p2
                    for kc in range(8):
                        S.op("pe", MM(pb[bg][:, 0:256], Wg[:, kc, fc * 128:(fc + 1) * 128], u2Tq[:, kc, :], start=(kc == 0), stop=(kc == 7)),
                             reads=["Wg", U2T[kc]], writes=["P%d" % bg])
                    for kc in range(8):
                        S.op("pe", MM(pb[bu][:, 0:256], Wu[:, kc, fc * 128:(fc + 1) * 128], u2Tq[:, kc, :], start=(kc == 0), stop=(kc == 7)),
                             reads=["Wu", U2T[kc]], writes=["P%d" % bu])
                    S.op("act", ACTF(sg[p2], pb[bg][:, 0:256], AF.Silu), reads=["P%d" % bg], writes=["sg%d" % p2])
                    S.op("dve", TT(actT[:, fc, :], sg[p2], pb[bu][:, 0:256], ALU.mult), reads=["sg%d" % p2, "P%d" % bu], writes=["actT%d" % fc])
                if g + 1 < NG:
                    prepB(g + 1)
                AT = ["actT%d" % fc for fc in range(NFC)]
                for j in range(2):
                    for half in range(2):
                        q = 1 + (2 * j + half) % 2
                        for fc in range(NFC):
                            S.op("pe", MM(pb[q], actT[:, fc, j * 128:(j + 1) * 128], Wd[:, fc, half * 512:(half + 1) * 512],
                                          start=(fc == 0), stop=(fc == NFC - 1)), reads=[AT[fc], "Wd"], writes=["P%d" % q])
                        sl = hbq[:, j, half * 512:(half + 1) * 512]
                        S.op("dve", TT(sl, sl, pb[q], ALU.add), reads=["P%d" % q, "hb%d" % q_], writes=["hb%d" % q_])
                    S.op("pool", MS(st[:, 4:5], 0.0), writes=["st2"])
                    S.op("act", ACTF(junk2, hbq[:, j, :], AF.Square, accum_out=st[:, 4:5]), reads=["hb%d" % q_], writes=["junk2", "st2"])
                    S.op("dve", TS(st[:, 5:6], st[:, 4:5], 1.0 / D, EPS, ALU.mult, ALU.add), reads=["st2"], writes=["st2"])
                    S.op("act", ACTF(st[:, 6:7], st[:, 5:6], AF.Ln), reads=["st2"], writes=["st2"])
                    S.op("act", ACTF(st[:, 7:8], st[:, 6:7], AF.Exp, scale=-0.5), reads=["st2"], writes=["st2"])
                    S.op("dve", STT(hbq[:, j, :], hbq[:, j, :], st[:, 7:8], finw, ALU.mult, ALU.mult), reads=["hb%d" % q_, "st2", "constB"], writes=["hb%d" % q_])
                S.op("sp", DMA(out_d[g * 256:(g + 1) * 256, :].rearrange("(t p) d -> p t d", p=128), hbq),
                     reads=["hb%d" % q_], writes=["outd"], dma="ost%d" % q_)

        fw = []
        for nm_ in ("ost0", "ost1", "h1st0", "h1st1", "T_dump"):
            if nm_ in S.dsem:
                fw.append(nm_)
        with nc.Block() as block:
            S.emit(block, final_waits=fw)
    return nc


def make_consts():
    p = np.arange(128)[:, None]
    f = np.arange(128)[None, :]
    identf = (p == f).astype(np.float32)
    tri = (p <= f).astype(np.float32)
    ones = np.ones((128, 128), np.float32)
    half = np.broadcast_to((p < 64), (128, 128)).astype(np.float32)
    posa = np.where(f < p, 0.0, 30000.0).astype(np.float32)
    negd = np.where(p <= f, 0.0, -30000.0).astype(np.float32)
    spare = np.zeros((128, 128), np.float32)
    cf = np.concatenate([identf, tri, ones, half, posa, negd, spare], axis=1)
    identb = identf.astype(ml_dtypes.bfloat16)
    mask01 = (p <= f).astype(np.float32).astype(ml_dtypes.bfloat16)
    cb = np.concatenate([identb, mask01], axis=1)
    return np.ascontiguousarray(cf), np.ascontiguousarray(cb)


def make_in_maps(x, meta_tokens, attn_norm_w, w_in, conv_w, a_log, dt_bias, gdn_norm_w, fgate_b,
                 w_out, ffn_norm_w, w_gate, w_up, w_down, final_norm_w):
    f = lambda a: np.ascontiguousarray(np.asarray(a, dtype=np.float32))
    cf, cb = make_consts()
    convw = f(np.asarray(conv_w)[0].reshape(4, 12, 128).transpose(2, 1, 0).reshape(128, 48))
    anw = f(np.asarray(attn_norm_w)[0].reshape(8, 128).T)
    fnw = f(np.asarray(ffn_norm_w)[0].reshape(8, 128).T)
    finw = f(np.broadcast_to(np.asarray(final_norm_w)[None, :], (128, D)))
    gnw = f(np.broadcast_to(np.tile(np.asarray(gdn_norm_w)[0], 4)[None, :], (128, 512)))
    raw12 = f(np.broadcast_to(np.concatenate([np.asarray(dt_bias)[0], np.asarray(fgate_b)[0]])[None, :], (128, 12)))
    sgn12 = f(np.broadcast_to(np.array([1.0] * 4 + [-1.0] * 8, np.float32)[None, :], (128, 12)))
    alog = f(np.broadcast_to(np.asarray(a_log)[0][None, :], (128, 4)))
    shared = dict(meta=f(meta_tokens), w_in=f(np.asarray(w_in)[0]), w_out=f(np.asarray(w_out)[0]),
                  w_gate=f(np.asarray(w_gate)[0]), w_up=f(np.asarray(w_up)[0]), w_down=f(np.asarray(w_down)[0]),
                  cf32=cf, cbf16=cb, convw=convw, anw=anw, fnw=fnw, finw=finw, gnw=gnw, raw12=raw12, sgn12=sgn12, alog=alog)
    x = np.asarray(x, dtype=np.float32)
    maps = []
    for b in range(x.shape[0]):
        m = dict(shared)
        m["x"] = np.ascontiguousarray(x[b])
        maps.append(m)
    return maps


def kernel(**inputs):
    x = np.asarray(inputs["x"])
    B, L, _ = x.shape
    NR = L // 128
    maps = make_in_maps(**inputs)
    nc = build_nc(NR)
    res = run_bass_kernel_spmd(nc, maps, core_ids=list(range(B)))
    out = np.stack([np.asarray(res.results[b]["out"]).reshape(L, D) for b in range(B)], axis=0)
    return out.astype(np.float32)
```
